# Optimizing a Trainium2 kernel written in Bass

```python
import jax, jax.numpy as jnp
from jax import lax
import numpy as np

D_MODEL = 1024
BATCH = 8
SEQ = 2048
DEPTH = 4

N_META = 16
EXPAND = 2
D_INNER = EXPAND * D_MODEL
POOL_WINDOWS = (2, 4, 8, 16)
N_POOL_GROUPS = len(POOL_WINDOWS)
POOL_GROUP = D_INNER // N_POOL_GROUPS
GLA_HEADS = 4
GLA_DK_TOTAL = D_INNER // 2
GLA_DK = GLA_DK_TOTAL // GLA_HEADS
GLA_DV = D_INNER // GLA_HEADS
GATE_RANK = 16
GATE_TAU = 16.0
CHUNK = 64
EPS = 1e-6
N_POOL_LAYERS = (DEPTH + 1) // 2
N_GLA_LAYERS = DEPTH // 2
GLA_IN = 2 * GLA_DK_TOTAL + 2 * D_INNER + GATE_RANK

kernel_name = "hybrid_pool_gla_meta_trunk"


def rmsnorm(x, g):
    xf = x.astype(jnp.float32)
    y = xf * lax.rsqrt(jnp.mean(xf * xf, axis=-1, keepdims=True) + EPS)
    return (y * g.astype(jnp.float32)).astype(x.dtype)


def pool_mixer(h, w_in, w_grp, scale, w_out):
    B, L, _ = h.shape
    u, z = jnp.split(h @ w_in, 2, axis=-1)
    uf = u.astype(jnp.float32).reshape(B, L, N_POOL_GROUPS, POOL_GROUP)
    c = jnp.cumsum(uf, axis=1)
    pos = jnp.arange(1, L + 1, dtype=jnp.float32)
    pooled = []
    for gi, w in enumerate(POOL_WINDOWS):
        cg = c[:, :, gi]
        shifted = jnp.pad(cg, ((0, 0), (w, 0), (0, 0)))[:, :L]
        cnt = jnp.minimum(pos, float(w))[None, :, None]
        pooled.append((cg - shifted) / cnt - uf[:, :, gi])
    t = jnp.stack(pooled, axis=2).astype(h.dtype)
    y = jnp.einsum('blgc,gcd->blgd', t, w_grp).reshape(B, L, D_INNER)
    y = y * scale * jax.nn.silu(z)
    return y @ w_out


def gla_mixer(h, w_in, w_gate2, b_gate, head_norm, w_out):
    B, L, _ = h.shape
    proj = h @ w_in
    q, k, v, z, glr = jnp.split(
        proj, [GLA_DK_TOTAL, 2 * GLA_DK_TOTAL, 2 * GLA_DK_TOTAL + D_INNER, 2 * GLA_DK_TOTAL + 2 * D_INNER],
        axis=-1)
    log_a = jax.nn.log_sigmoid((glr @ w_gate2 + b_gate).astype(jnp.float32)) / GATE_TAU

    pad = (-N_META) % CHUNK
    Lp = L + pad
    n_chunks = Lp // CHUNK

    def to_chunks(t, d):
        t = t.astype(jnp.float32).reshape(B, L, GLA_HEADS, d)
        t = jnp.pad(t, ((0, 0), (pad, 0), (0, 0), (0, 0)))
        return t.reshape(B, n_chunks, CHUNK, GLA_HEADS, d).transpose(1, 0, 3, 2, 4)

    qc = to_chunks(q, GLA_DK) * (GLA_DK ** -0.5)
    kc = to_chunks(k, GLA_DK)
    vc = to_chunks(v, GLA_DV)
    lac = to_chunks(log_a, GLA_DK)
    mask = jnp.tril(jnp.ones((CHUNK, CHUNK), dtype=bool))

    def step(S, inp):
        qb, kb, vb, lab = inp
        cum = jnp.cumsum(lab, axis=2)
        qe = qb * jnp.exp(cum)
        ke = kb * jnp.exp(-cum)
        att = jnp.where(mask, jnp.einsum('bhtd,bhsd->bhts', qe, ke), 0.0)
        o = jnp.einsum('bhts,bhse->bhte', att, vb) + jnp.einsum('bhtd,bhde->bhte', qe, S)
        g_tot = cum[:, :, -1]
        kd = kb * jnp.exp(g_tot[:, :, None, :] - cum)
        S_new = jnp.exp(g_tot)[..., None] * S + jnp.einsum('bhsd,bhse->bhde', kd, vb)
        return S_new, o

    S0 = jnp.zeros((B, GLA_HEADS, GLA_DK, GLA_DV), jnp.float32)
    _, o = lax.scan(step, S0, (qc, kc, vc, lac))
    o = o.transpose(1, 0, 3, 2, 4).reshape(B, Lp, GLA_HEADS, GLA_DV)[:, pad:]
    o = o * lax.rsqrt(jnp.mean(o * o, axis=-1, keepdims=True) + EPS)
    o = o.reshape(B, L, D_INNER) * head_norm.astype(jnp.float32)
    y = o.astype(h.dtype) * jax.nn.silu(z)
    return y @ w_out


def setup_inputs(seed: int = 0) -> dict:
    key = jax.random.key(seed)
    ks = jax.random.split(key, 16)
    f32 = jnp.float32
    nrm = lambda k, shape, s: jax.random.normal(k, shape, f32) * s
    return {
        "x": nrm(ks[0], (BATCH, SEQ, D_MODEL), 1.0),
        "meta": nrm(ks[1], (N_META, D_MODEL), 1.0),
        "final_norm": 1.0 + nrm(ks[2], (D_MODEL,), 0.02),
        "pool_norm": 1.0 + nrm(ks[3], (N_POOL_LAYERS, D_MODEL), 0.02),
        "pool_w_in": nrm(ks[4], (N_POOL_LAYERS, D_MODEL, 2 * D_INNER), D_MODEL ** -0.5),
        "pool_w_grp": nrm(ks[5], (N_POOL_LAYERS, N_POOL_GROUPS, POOL_GROUP, POOL_GROUP), POOL_GROUP ** -0.5),
        "pool_scale": 1.0 + nrm(ks[6], (N_POOL_LAYERS, D_INNER), 0.02),
        "pool_w_out": nrm(ks[7], (N_POOL_LAYERS, D_INNER, D_MODEL), D_INNER ** -0.5),
        "gla_norm": 1.0 + nrm(ks[8], (N_GLA_LAYERS, D_MODEL), 0.02),
        "gla_w_in": nrm(ks[9], (N_GLA_LAYERS, D_MODEL, GLA_IN), D_MODEL ** -0.5),
        "gla_w_gate2": nrm(ks[10], (N_GLA_LAYERS, GATE_RANK, GLA_DK_TOTAL), GATE_RANK ** -0.5),
        "gla_b_gate": nrm(ks[11], (N_GLA_LAYERS, GLA_DK_TOTAL), 0.1),
        "gla_head_norm": 1.0 + nrm(ks[12], (N_GLA_LAYERS, D_INNER), 0.02),
        "gla_w_out": nrm(ks[13], (N_GLA_LAYERS, D_INNER, D_MODEL), D_INNER ** -0.5),
    }


def reference(x, meta, final_norm, pool_norm, pool_w_in, pool_w_grp, pool_scale, pool_w_out,
              gla_norm, gla_w_in, gla_w_gate2, gla_b_gate, gla_head_norm, gla_w_out):
    B = x.shape[0]
    h = jnp.concatenate([jnp.broadcast_to(meta.astype(x.dtype)[None], (B, N_META, D_MODEL)), x], axis=1)
    for i in range(DEPTH):
        j = i // 2
        if i % 2 == 0:
            h = h + pool_mixer(rmsnorm(h, pool_norm[j]), pool_w_in[j], pool_w_grp[j],
                               pool_scale[j], pool_w_out[j])
        else:
            h = h + gla_mixer(rmsnorm(h, gla_norm[j]), gla_w_in[j], gla_w_gate2[j],
                              gla_b_gate[j], gla_head_norm[j], gla_w_out[j])
    return rmsnorm(h, final_norm)[:, N_META:]
```

```python
from contextlib import ExitStack
import numpy as np
import concourse.bass as bass
import concourse.mybir as mybir
from concourse.bass_utils import run_bass_kernel_spmd

F32 = mybir.dt.float32
BF16 = mybir.dt.bfloat16
AF = mybir.ActivationFunctionType
ALU = mybir.AluOpType

D = 1024
SEQ = 2048
NMETA = 16
L = SEQ + NMETA
NT = 17
DI = 2048
WINS = (2, 4, 8, 16)
GLA_IN = 6160
EPS = 1e-6
ENGS = ("pe", "act", "dve", "pool", "sp")


class Buf:
    __slots__ = ("name", "lw", "rd", "psum")

    def __init__(self, name, psum=False):
        self.name = name
        self.lw = None
        self.rd = {}
        self.psum = psum


class DSem:
    def __init__(self, name):
        self.name = name
        self.count = 0
        self.handle = None


class Prog:
    def __init__(self):
        self.q = {e: [] for e in ENGS}
        self.cnt = {e: 0 for e in ENGS}
        self.seen = {e: {} for e in ENGS}
        self.dsems = {}

    def dsem(self, name):
        s = DSem("d_" + name)
        self.dsems[s.name] = s
        return s

    def _deps(self, eng, reads, writes, ignore=None):
        w = {}

        def need(dep):
            if dep is None or dep == ignore:
                return
            k, v = dep
            if self.seen[eng].get(k, 0) >= v:
                return
            if w.get(k, 0) < v:
                w[k] = v

        for b in reads:
            need(b.lw)
            if b.psum:
                for k, v in b.rd.items():
                    if k != eng:
                        need((k, v))
        for b in writes:
            need(b.lw)
            for k, v in b.rd.items():
                need((k, v))
        for k, v in w.items():
            self.seen[eng][k] = v
        return w

    def op(self, eng, fn, reads=(), writes=()):
        w = self._deps(eng, reads, writes)
        idx = self.cnt[eng] + 1
        self.cnt[eng] = idx
        self.q[eng].append((w, fn, True, None))
        for b in reads:
            b.rd[eng] = idx
        for b in writes:
            b.lw = (eng, idx)
            b.rd = {}

    def dma(self, queue, sem, items):
        pre = {}
        if sem.count > 0 and self.seen[queue].get(sem.name, 0) < sem.count:
            pre[sem.name] = sem.count
            self.seen[queue][sem.name] = sem.count
        final = sem.count + 16 * len(items)
        first = True
        for fn, reads, writes in items:
            w = self._deps(queue, reads, writes, ignore=(sem.name, final))
            if first:
                for k, v in pre.items():
                    if w.get(k, 0) < v:
                        w[k] = v
                first = False
            self.q[queue].append((w, fn, False, sem))
            for b in reads:
                b.rd[sem.name] = final
            for b in writes:
                b.lw = (sem.name, final)
                b.rd = {}
        sem.count = final

    def barrier(self, engs=("pe", "act", "dve")):
        for e in engs:
            w = {}
            for e2 in engs:
                v = self.cnt[e2]
                if v > 0 and self.seen[e].get(e2, 0) < v:
                    w[e2] = v
                    self.seen[e][e2] = v
            if w:
                self.q[e].append((w, None, False, None))

    def wait_all(self, eng, bufs):
        w = self._deps(eng, [], bufs)
        self.q[eng].append((w, None, False, None))

    def emit(self, nc, E):
        esem = {e: E(nc.semaphore("s_" + e)) for e in ENGS}
        for s in self.dsems.values():
            s.handle = E(nc.semaphore(s.name))

        def semof(k):
            return esem[k] if k in esem else self.dsems[k].handle

        block = E(nc.Block())

        def replay(e, h):
            for (w, fn, signal, dsem) in self.q[e]:
                for k, v in w.items():
                    h.wait_ge(semof(k), v)
                if fn is None:
                    continue
                ins = fn(h)
                if dsem is not None:
                    ins.then_inc(dsem.handle, 16)
                elif signal:
                    ins.then_inc(esem[e], 1)

        @block.tensor
        def _(h):
            replay("pe", h)

        @block.scalar
        def _(h):
            replay("act", h)

        @block.vector
        def _(h):
            replay("dve", h)

        @block.gpsimd
        def _(h):
            replay("pool", h)

        @block.sync
        def _(h):
            replay("sp", h)


def host_consts():
    c = np.zeros((128, 512), np.float32)
    c[:, 0:128] = np.eye(128, dtype=np.float32)
    c[:, 128:256] = np.triu(np.ones((128, 128), np.float32))
    inv = np.zeros((4, 4, 16), np.float32)
    for g, w in enumerate(WINS):
        for t in range(16):
            inv[g, :, t] = 1.0 / min(t + 1, w)
    c[:, 256:512] = inv.reshape(1, 256)
    return c


def build(n_layers=4):
    nc = bass.Bass("TRN2", target_bir_lowering=False)
    dt = lambda name, shape: nc.dram_tensor(name, shape, F32, kind="ExternalInput").ap()
    x = dt("x", [128, 16 * D])
    meta = dt("meta", [NMETA, D])
    final_norm = dt("final_norm", [1, D])
    pool_norm = dt("pool_norm", [2, D])
    wpack = dt("wpack", [16, 128, 16384])
    vecp = dt("vecp", [128, 64])
    wglrp = dt("wglrp", [2, 128, 128])
    gla_norm = dt("gla_norm", [2, D])
    gla_w_gate2 = dt("gla_w_gate2", [2, 16, 1024])
    gla_b_gate = dt("gla_b_gate", [2, 1024])
    cst = dt("cst", [128, 512])
    out = nc.dram_tensor("out", [SEQ, D], F32, kind="ExternalOutput").ap()

    P = Prog()
    es = ExitStack()
    E = es.enter_context
    sb = lambda name, shape, dtype: E(nc.sbuf_tensor(name, shape, dtype))

    h = sb("h", [128, NT, D], F32)
    hnT = sb("hnT", [128, 8, L], BF16)
    Wt = [sb("W0", [128, 16384], BF16), sb("W1", [128, 16384], BF16)]
    ident = sb("ident", [128, 128], BF16)
    tri = sb("tri", [128, 128], BF16)
    ones = sb("ones", [128, 128], BF16)
    invcf = sb("invcf", [128, 256], F32)
    pscale = sb("pscale", [128, 16], F32)
    hnorm = sb("hnorm", [128, 16], F32)
    epsc = sb("epsc", [128, 1], F32)
    ssb = sb("ssb", [128, NT], F32)
    lnv = sb("lnv", [128, NT], F32)
    rstd = sb("rstd", [128, NT], F32)
    Wglr = sb("Wglr", [128, 8, 16], BF16)
    wg2 = sb("wg2", [64, 1024], BF16)
    glrT = sb("glrT", [64, L], BF16)
    gbc = sb("gbc", [128, D], F32)
    hnb = sb("hnb", [128, D], BF16)
    invc = invcf[:, :].rearrange("p (g f t) -> p g f t", g=4, f=4)

    ARENA = 29696
    arena = sb("arena", [128, ARENA // 4], F32)
    _off = [0]

    def carve(nbytes, dtype, pat=None, **kw):
        a = _off[0]
        assert a % 4 == 0 and nbytes % 4 == 0
        _off[0] = a + nbytes
        assert _off[0] <= ARENA, _off[0]
        ap = arena[:, a // 4:(a + nbytes) // 4]
        if dtype == BF16:
            ap = ap.bitcast(BF16)
        if pat:
            ap = ap.rearrange(pat, **kw)
        return ap

    PS = [E(nc.psum_tensor("ps%d" % i, [128, 1024], F32)) for i in range(4)]
    bank = lambda i: PS[i // 2][:, (i % 2) * 512:(i % 2) * 512 + 512]
    pbl = [[Buf("bank%d" % i, psum=True)] for i in range(8)]
    b3a = b3b = pbl[3][0]
    b4a = b4c = pbl[4][0]
    b4b = pbl[2][0]
    PB = lambda *idx: [b for i in idx for b in pbl[i]]

    hb = [Buf("h%d" % i) for i in range(NT)]
    hnTb = [Buf("hnT%d" % i) for i in range(NT)]
    Wb = [[Buf("W%d_%d" % (s_, p_)) for p_ in range(3)] for s_ in range(2)]
    Wsem = [[P.dsem("W%d_%d" % (s_, p_)) for p_ in range(3)] for s_ in range(2)]
    b_const = Buf("const")
    b_gbc = Buf("gbc")
    b_hnb = Buf("hnb")
    b_vec = Buf("vec")
    b_wglr = Buf("wglr")
    b_wg2 = Buf("wg2")
    b_wg2b = Buf("wg2b")
    b_glrT = [Buf("glrT%d" % i) for i in range(NT)]
    b_stat = [Buf("stat%d" % i) for i in range(NT)]
    s_const = P.dsem("const")
    s_x = [P.dsem("x%d" % i) for i in range(16)]
    s_meta = P.dsem("meta")
    s_gbc = P.dsem("gbc")
    s_vec = P.dsem("vec")
    s_wglr = P.dsem("wglr")
    s_wg2 = P.dsem("wg2")
    s_wg2b = P.dsem("wg2b")
    s_out = P.dsem("out")

    tile_rows = lambda i: NMETA if i == 0 else 128
    tile_col0 = lambda i: 0 if i == 0 else NMETA + (i - 1) * 128

    P.dma("pool", s_const, [(lambda q: q.dma_start(out=arena[:, 0:256], in_=cst[:, 0:256]), [], [b_const]),
                          (lambda q: q.dma_start(out=invcf[:], in_=cst[:, 256:512]), [], [b_const])])
    P.op("dve", lambda e: e.tensor_copy(out=ident[:], in_=arena[:, 0:128]), [b_const], [b_const])
    P.op("dve", lambda e: e.tensor_copy(out=tri[:], in_=arena[:, 128:256]), [b_const], [b_const])
    P.op("dve", lambda e: e.memset(ones[:], 1.0), [], [b_const])
    P.op("dve", lambda e: e.memset(epsc[:], EPS), [], [b_const])
    def load_x():
        P.dma("sp", s_meta, [(lambda q: q.dma_start(out=h[0:NMETA, 0, :], in_=meta[:, :]), [], [hb[0]])])
        xv = x.rearrange("p (t d) -> p t d", t=16)
        for k in range(16):
            P.dma("sp", s_x[k],
                  [(lambda q, k=k: q.dma_start(out=h[:, 1 + k:2 + k, :], in_=xv[:, k:k + 1, :]), [], [hb[1 + k]])])

    units = []
    for li in range(n_layers):
        for u in range(4):
            units.append((li, u))

    def load_unit(ui):
        li, u = units[ui]
        s = ui % 2
        W = Wt[s]
        j = li // 2
        items = [[], [], []]
        src = wpack[ui]
        rng = [(0, 4096), (4096, 8192)], [(8192, 10240) if li % 2 == 0 else (8192, 12288)], [(12288, 16384)]
        for p_ in range(3):
            for (c0_, c1_) in rng[p_]:
                items[p_].append((lambda q, c0_=c0_, c1_=c1_: q.dma_start(out=W[:, c0_:c1_], in_=src[:, c0_:c1_]), [], [Wb[s][p_]]))
        for p_ in range(3):
            it = items[p_]
            if not it:
                continue
            if ui == 1 or (ui == 0 and p_ >= 1):
                it = [(fn, list(r) + [hb[NT - 1]], w) for fn, r, w in it]
            P.dma("pool", Wsem[s][p_], it)

    x_loaded = [False]
    sq_ready = [False]

    def stats_sq(i, eng):
        n = tile_rows(i)
        c0 = tile_col0(i)
        if i == 0:
            jout, src, jb = hnb[0:n, :], h[0:n, 0, :], [b_hnb]
        else:
            jout, src, jb = hnT[:, :, c0:c0 + 128], h[:, i, :].rearrange("p (k m) -> p k m", k=8), [hnTb[i]]
        if eng == "act":
            P.op("act", lambda e: e.activation(out=jout, in_=src, func=AF.Square, accum_out=ssb[0:n, i:i + 1]),
                 [hb[i]], jb + [b_stat[i]])
        else:
            P.op("dve", lambda e: e.scalar_tensor_tensor(out=jout, in0=src, scalar=1.0, in1=src, op0=ALU.mult, op1=ALU.mult,
                                                         accum_out=ssb[0:n, i:i + 1]), [hb[i]], jb + [b_stat[i]])

    def prepass(gvec_ap, gla):
        P.dma("sp", s_gbc, [(lambda q: q.dma_start(out=gbc[:], in_=gvec_ap.partition_broadcast(128)), [], [b_gbc])])
        if not x_loaded[0]:
            load_x()
            x_loaded[0] = True
        P.barrier()
        hnbs = [hnb, arena[:, 0:512].bitcast(BF16)]
        b_hnbs = [b_hnb, Buf("hnb2")]
        junk = PS[3]

        junk2 = PS[2]

        have_sq = sq_ready[0]
        sq_ready[0] = False

        if have_sq:
            for (r_, c0_, c1_) in ((NMETA, 0, 1), (128, 1, NT)):
                bs_ = [b_stat[t_] for t_ in range(c0_, c1_)]
                P.op("act", lambda e, r_=r_, c0_=c0_, c1_=c1_: e.activation(out=lnv[0:r_, c0_:c1_], in_=ssb[0:r_, c0_:c1_], func=AF.Ln,
                                                                         scale=1.0 / D, bias=epsc[0:r_, 0:1]), bs_ + [b_const], bs_)
                P.op("act", lambda e, r_=r_, c0_=c0_, c1_=c1_: e.activation(out=rstd[0:r_, c0_:c1_], in_=lnv[0:r_, c0_:c1_], func=AF.Exp,
                                                                         scale=-0.5), bs_, bs_)

        def stats(i):
            n = tile_rows(i)
            if have_sq:
                return
            elif i % 2 == 0:
                P.op("act", lambda e: e.activation(out=junk[0:n, :], in_=h[0:n, i, :], func=AF.Square, accum_out=ssb[0:n, i:i + 1]),
                     [hb[i]], PB(6, 7) + [b_stat[i]])
            else:
                P.op("dve", lambda e: e.scalar_tensor_tensor(out=junk2[0:n, :], in0=h[0:n, i, :], scalar=1.0, in1=h[0:n, i, :],
                                                             op0=ALU.mult, op1=ALU.mult, accum_out=ssb[0:n, i:i + 1]),
                     [hb[i]], PB(4, 5) + [b_stat[i]])
            P.op("act", lambda e: e.activation(out=lnv[0:n, i:i + 1], in_=ssb[0:n, i:i + 1], func=AF.Ln, scale=1.0 / D,
                                               bias=epsc[0:n, 0:1]), [b_stat[i], b_const], [b_stat[i]])
            P.op("act", lambda e: e.activation(out=rstd[0:n, i:i + 1], in_=lnv[0:n, i:i + 1], func=AF.Exp, scale=-0.5),
                 [b_stat[i]], [b_stat[i]])

        def body(i):
            n = tile_rows(i)
            c0 = tile_col0(i)
            hb_ = hnbs[i % 2]
            bb_ = b_hnbs[i % 2]
            P.op("dve", lambda e: e.scalar_tensor_tensor(out=hb_[0:n, :], in0=h[0:n, i, :], scalar=rstd[0:n, i:i + 1],
                                                         in1=gbc[0:n, :], op0=ALU.mult, op1=ALU.mult),
                 [hb[i], b_stat[i], b_gbc], [bb_])
            pb = i % 2
            pT = bank(pb).bitcast(BF16).rearrange("p (kc n) -> p kc n", kc=8)

            def tr(e):
                ins = None
                for kc in range(8):
                    ins = e.transpose(out=pT[:, kc, 0:n], in_=hb_[0:n, kc * 128:(kc + 1) * 128], identity=ident[0:n, 0:n])
                return ins
            P.op("pe", tr, [bb_, b_const], PB(pb))
            return pT

        def evac(i, pT):
            n = tile_rows(i)
            c0 = tile_col0(i)
            pb = i % 2
            P.op("act", lambda e: e.activation(out=hnT[:, :, c0:c0 + n], in_=pT[:, :, 0:n], func=AF.Copy), PB(pb), [hnTb[i]])

        def glr_mm(i):
            n = tile_rows(i)
            c0 = tile_col0(i)
            pb = i % 2
            pg = bank(2 + pb)

            def glr(e):
                ins = None
                for kc in range(8):
                    ins = e.matmul(pg[0:16, 0:n], lhsT=Wglr[:, kc, :], rhs=hnT[:, kc, c0:c0 + n], start=(kc == 0), stop=(kc == 7))
                return ins
            P.op("pe", glr, [hnTb[i], b_wglr], PB(2 + pb))

        def glr_evac(i):
            n = tile_rows(i)
            c0 = tile_col0(i)
            pg = bank(2 + i % 2)
            P.op("act", lambda e: e.activation(out=glrT[0:16, c0:c0 + n], in_=pg[0:16, 0:n], func=AF.Copy),
                 PB(2 + i % 2), [b_glrT[i]])

        stats(0)
        stats(1)
        for i in range(NT):
            pT = body(i)
            if i + 2 < NT:
                stats(i + 2)
            evac(i, pT)
            if gla and i >= 1:
                glr_mm(i - 1)
            if gla and i >= 2:
                glr_evac(i - 2)
        if gla:
            glr_mm(NT - 1)
            glr_evac(NT - 2)
            glr_evac(NT - 1)

    def pool_layer(li, ui0):
        j = li // 2
        prepass(pool_norm[j:j + 1, :], False)
        P.dma("sp", s_vec, [(lambda q: q.dma_start(out=pscale[:], in_=vecp[:, 16 * j:16 * j + 16]), [], [b_vec])])
        P.barrier()
        _off[0] = 0
        NB = 256
        U = [carve(4 * 272 * 4, F32, "p (f m) -> p f m", f=4) for _ in range(2)]
        SA = carve(4 * 272 * 4, F32, "p (f m) -> p f m", f=4)
        SB = carve(4 * 272 * 4, F32, "p (f m) -> p f m", f=4)
        T = [carve(4 * NB * 2, BF16, "p (f m) -> p f m", f=4) for _ in range(2)]
        SZ = [carve(4 * NB * 2, BF16, "p (f m) -> p f m", f=4) for _ in range(2)]
        Y = [carve(4 * NB * 2, BF16, "p (f m) -> p f m", f=4) for _ in range(2)]
        bU = [Buf("U0"), Buf("U1")]
        bS = Buf("SAB")
        bT = [Buf("T0"), Buf("T1")]
        bSZ = [Buf("SZ0"), Buf("SZ1")]
        bY = [Buf("Y0"), Buf("Y1")]
        blocks = [(0, NMETA, [0])] + [(NMETA + NB * b, NB, [1 + 2 * b, 2 + 2 * b]) for b in range(SEQ // NB)]
        pu = PS[0].rearrange("p (f m) -> p f m", f=4)
        pz = PS[1].rearrange("p (f m) -> p f m", f=4)
        py = PS[0].rearrange("p (f m) -> p f m", f=4)

        def group_pass(g, ui):
            s = ui % 2
            W = Wt[s]
            w = WINS[g]
            Wuz = W[:, 0:8192].rearrange("p (kc n) -> p kc n", kc=8)
            Wg = W[:, 8192:10240].rearrange("p (fc n) -> p fc n", fc=4)
            Wo = W[:, 12288:16384].rearrange("p (oc n) -> p oc n", oc=4)

            fins = {}

            def stageA(bi):
                c0, n, tiles = blocks[bi]
                par = (g + bi) % 2
                Ub = U[par]
                rd_h = [hnTb[t] for t in tiles]

                def mm(e, off, dst):
                    ins = None
                    for fc in range(4):
                        for kc in range(8):
                            ins = e.matmul(dst[:, fc, 0:n], lhsT=Wuz[:, kc, off + fc * 128:off + (fc + 1) * 128],
                                           rhs=hnT[:, kc, c0:c0 + n], start=(kc == 0), stop=(kc == 7))
                    return ins
                P.op("pe", lambda e: mm(e, 0, pu), rd_h + [Wb[s][0]], PB(0, 1))
                if bi == 0:
                    P.op("act", lambda e: e.memzero(Ub[:, :, 0:16]), [], [bU[par]])
                else:
                    pn = blocks[bi - 1][1]
                    Up = U[1 - par]
                    P.op("act", lambda e: e.activation(out=Ub[:, :, 0:16], in_=Up[:, :, pn:pn + 16], func=AF.Copy),
                         [bU[1 - par]], [bU[par]])
                P.op("act", lambda e: e.activation(out=Ub[:, :, 16:16 + n], in_=pu[:, :, 0:n], func=AF.Copy),
                     PB(0, 1), [bU[par]])
                P.op("pe", lambda e: mm(e, 512, pz), rd_h + [Wb[s][0]], PB(2, 3))
                P.op("act", lambda e: e.activation(out=SZ[par][:, :, 0:n], in_=pz[:, :, 0:n], func=AF.Silu),
                     PB(2, 3), [bSZ[par]])
                e_ = 16 + n
                add = lambda dst, a, b: (lambda e: e.tensor_tensor(out=dst, in0=a, in1=b, op=ALU.add))
                AE = "pool"
                if w == 2:
                    P.op(AE, add(SA[:, :, 16:e_], Ub[:, :, 16:e_], Ub[:, :, 15:e_ - 1]), [bU[par]], [bS])
                    fin = SA
                elif w == 4:
                    P.op(AE, add(SA[:, :, 14:e_], Ub[:, :, 14:e_], Ub[:, :, 13:e_ - 1]), [bU[par]], [bS])
                    P.op(AE, add(SB[:, :, 16:e_], SA[:, :, 16:e_], SA[:, :, 14:e_ - 2]), [bS], [bS])
                    fin = SB
                elif w == 8:
                    P.op(AE, add(SA[:, :, 10:e_], Ub[:, :, 10:e_], Ub[:, :, 9:e_ - 1]), [bU[par]], [bS])
                    P.op(AE, add(SB[:, :, 12:e_], SA[:, :, 12:e_], SA[:, :, 10:e_ - 2]), [bS], [bS])
                    P.op(AE, add(SA[:, :, 16:e_], SB[:, :, 16:e_], SB[:, :, 12:e_ - 4]), [bS], [bS])
                    fin = SA
                else:
                    P.op(AE, add(SA[:, :, 2:e_], Ub[:, :, 2:e_], Ub[:, :, 1:e_ - 1]), [bU[par]], [bS])
                    P.op(AE, add(SB[:, :, 4:e_], SA[:, :, 4:e_], SA[:, :, 2:e_ - 2]), [bS], [bS])
                    P.op(AE, add(SA[:, :, 8:e_], SB[:, :, 8:e_], SB[:, :, 4:e_ - 4]), [bS], [bS])
                    P.op(AE, add(SB[:, :, 16:e_], SA[:, :, 16:e_], SA[:, :, 8:e_ - 8]), [bS], [bS])
                    fin = SB
                fins[bi] = fin

            def stageA2(bi):
                c0, n, tiles = blocks[bi]
                par = (g + bi) % 2
                Ub = U[par]
                e_ = 16 + n
                fin = fins[bi]
                if bi == 0:
                    P.op("dve", lambda e: e.tensor_tensor(out=fin[:, :, 16:e_], in0=fin[:, :, 16:e_], in1=invc[:, g, :, :],
                                                          op=ALU.mult), [bS, b_const], [bS])
                    P.op("dve", lambda e: e.tensor_tensor(out=T[par][:, :, 0:n], in0=fin[:, :, 16:e_], in1=Ub[:, :, 16:e_],
                                                          op=ALU.subtract), [bS, bU[par]], [bT[par]])
                else:
                    for fc in range(4):
                        P.op("dve", lambda e, fc=fc: e.scalar_tensor_tensor(out=T[par][:, fc, 0:n], in0=fin[:, fc, 16:e_],
                                                                            scalar=1.0 / w, in1=Ub[:, fc, 16:e_],
                                                                            op0=ALU.mult, op1=ALU.subtract),
                             [bS, bU[par]], [bT[par]])

            def stageB(bi):
                c0, n, tiles = blocks[bi]
                par = (g + bi) % 2

                def mmg(e):
                    ins = None
                    for oc in range(4):
                        for fc in range(4):
                            ins = e.matmul(py[:, oc, 0:n], lhsT=Wg[:, fc, oc * 128:(oc + 1) * 128], rhs=T[par][:, fc, 0:n],
                                           start=(fc == 0), stop=(fc == 3))
                    return ins
                P.op("pe", mmg, [bT[par], Wb[s][1]], PB(0, 1))
                for oc in range(4):
                    P.op("dve", lambda e, oc=oc: e.scalar_tensor_tensor(out=Y[par][:, oc, 0:n], in0=py[:, oc, 0:n],
                                                                        scalar=pscale[:, g * 4 + oc:g * 4 + oc + 1],
                                                                        in1=SZ[par][:, oc, 0:n], op0=ALU.mult, op1=ALU.mult),
                         PB(0, 1) + [bSZ[par], b_vec], [bY[par]])

            def stageB2(bi):
                c0, n, tiles = blocks[bi]
                par = (g + bi) % 2
                for k, ti in enumerate(tiles):
                    nt = tile_rows(ti)
                    cc = k * 128
                    for half in range(2):
                        pbi = 4 + 2 * (k % 2) + half
                        po = bank(pbi)

                        def mmo(e, half=half, cc=cc, nt=nt, po=po):
                            ins = None
                            for oc in range(4):
                                ins = e.matmul(po[0:nt, :], lhsT=Y[par][:, oc, cc:cc + nt], rhs=Wo[:, oc, half * 512:(half + 1) * 512],
                                               start=(oc == 0), stop=(oc == 3))
                            return ins
                        P.op("pe", mmo, [bY[par], Wb[s][2]], PB(pbi))
                        P.op("dve", lambda e, half=half, nt=nt, ti=ti, po=po: e.tensor_tensor(
                            out=h[0:nt, ti, half * 512:(half + 1) * 512], in0=h[0:nt, ti, half * 512:(half + 1) * 512],
                            in1=po[0:nt, :], op=ALU.add), PB(pbi) + [hb[ti]], [hb[ti]])

            return dict(A=stageA, A2=stageA2, B=stageB, B2=stageB2, ui=ui)

        groups = [group_pass(g, ui0 + g) for g in range(4)]
        nb = len(blocks)
        steps = [(g, bi) for g in range(4) for bi in range(nb)]
        NS = len(steps)

        def call(name, k):
            g, bi = steps[k]
            groups[g][name](bi)

        def done_b2(k):
            call("B2", k)
            g, bi = steps[k]
            if g == 3 and li + 1 < n_layers:
                for t_ in blocks[bi][2]:
                    stats_sq(t_, "act")
            if bi == nb - 1 and groups[g]["ui"] + 2 < len(units):
                load_unit(groups[g]["ui"] + 2)

        for k in range(NS):
            call("A", k)
            if k >= 1:
                call("B", k - 1)
            if k >= 2:
                done_b2(k - 2)
            call("A2", k)
            if k == 3 and li + 1 < n_layers:
                gla_preload((li + 1) // 2)
        call("B", NS - 1)
        done_b2(NS - 2)
        done_b2(NS - 1)
        if li + 1 < n_layers:
            sq_ready[0] = True

    def gla_preload(j):
        P.dma("sp", s_vec, [(lambda q: q.dma_start(out=hnorm[:], in_=vecp[:, 32 + 16 * j:48 + 16 * j]), [], [b_vec])])
        wglr_src = wglrp[j].rearrange("p (kc n) -> p kc n", kc=8)
        P.dma("pool", s_wglr, [(lambda q: q.dma_start(out=Wglr[:, :, :], in_=wglr_src), [], [b_wglr])])
        P.op("act", lambda e: e.memzero(wg2[0:32, :]), [], [b_wg2])
        P.op("act", lambda e: e.memzero(wg2[32:64, :]), [], [b_wg2b])
        P.dma("pool", s_wg2, [(lambda q: q.dma_start(out=wg2[0:16, :], in_=gla_w_gate2[j]), [], [b_wg2])])
        P.dma("sp", s_gbc, [(lambda q: q.dma_start(out=gbc[32:33, :], in_=gla_b_gate[j:j + 1, :]), [], [b_gbc])])
        P.op("dve", lambda e: e.tensor_copy(out=wg2[32:33, :], in_=gbc[32:33, :]), [b_gbc], [b_wg2b])
        P.op("dve", lambda e: e.tensor_tensor(out=hnb[32:33, :], in0=gbc[32:33, :], in1=wg2[32:33, :], op=ALU.subtract),
             [b_gbc, b_wg2b], [b_hnb])
        P.dma("sp", s_wg2b, [(lambda q: q.dma_start(out=wg2[33:34, :], in_=hnb[32:33, :]), [b_hnb], [b_wg2b])])
        P.op("act", lambda e: e.memzero(glrT[0:32, :]), [], list(b_glrT))
        P.op("act", lambda e: e.memzero(glrT[32:64, :]), [], list(b_glrT))
        P.op("act", lambda e: e.activation(out=glrT[32:64, :], in_=glrT[32:64, :], func=AF.Copy, scale=0.0, bias=1.0),
             list(b_glrT), list(b_glrT))

    def gla_layer(li, ui0):
        j = li // 2
        prepass(gla_norm[j:j + 1, :], True)
        fuse_final = (li == n_layers - 1)
        if fuse_final:
            P.dma("sp", s_gbc, [(lambda q: q.dma_start(out=gbc[:], in_=final_norm[0:1, :].partition_broadcast(128)), [], [b_gbc])])
        P.barrier()
        _off[0] = 0
        v3 = dict(pat="p (f m) -> p f m")
        Eq = [carve(1024, F32, f=2, **v3) for _ in range(2)]
        Ek1 = carve(1024, F32, f=2, **v3)
        Ek = [Ek1, Ek1]
        la = carve(1024, F32)
        lhi = carve(512, BF16)
        llo = carve(512, BF16)
        rscol = [carve(16, F32) for _ in range(2)]
        qe = [carve(512, BF16, f=2, **v3) for _ in range(2)]
        ke = [carve(512, BF16, f=2, **v3) for _ in range(2)]
        kdT = [carve(512, BF16, f=2, **v3) for _ in range(2)]
        attm = [carve(256, BF16) for _ in range(2)]
        kd = [carve(512, BF16) for _ in range(2)]
        vb = [carve(1024, BF16) for _ in range(2)]
        o2 = carve(1024, BF16, f=4, **v3)
        sgs = [carve(2048, F32, f=4, **v3) for _ in range(2)]
        yT = [carve(1024, BF16, f=4, **v3) for _ in range(2)]
        rtmp = carve(16, F32)
        S = carve(4096, F32, f=2, **v3)
        Sb = carve(2048, BF16, f=2, **v3)
        bE = [Buf("E0"), Buf("E1")]
        b_Ek = Buf("Ek")
        bsg = [Buf("sg0"), Buf("sg1")]
        b_la, b_lh, b_nb = Buf("la"), Buf("lh"), Buf("nb")
        bQ = [Buf("Q0"), Buf("Q1")]
        bQk = [Buf("Qk0"), Buf("Qk1")]
        brc = [Buf("rc0"), Buf("rc1")]
        bA = [Buf("A0"), Buf("A1")]
        bK = [Buf("K0"), Buf("K1")]
        bV = [Buf("V0"), Buf("V1")]
        b_oT, b_o2, b_sg, b_rs, b_S, b_Sb = Buf("oT"), Buf("o2"), Buf("sg"), Buf("rs"), Buf("S"), Buf("Sb")
        bYt = [Buf("yT0"), Buf("yT1")]
        chunks = [(tile_col0(i), tile_rows(i), i) for i in range(NT)]
        pq = bank(0).rearrange("p (f m) -> p f m", f=4)
        pz = bank(1).rearrange("p (f m) -> p f m", f=4)
        pv = bank(2)
        pgate = bank(3)[:, 0:256]
        pcum = bank(3)[:, 256:512].rearrange("p (f m) -> p f m", f=2)
        patt = bank(4)[:, 0:128]
        pkdT = bank(2)[:, 0:128].bitcast(BF16)
        pss = bank(5)[:, 0:128]
        po = bank(5).rearrange("p (f m) -> p f m", f=4)
        pS = [bank(6), bank(7)]

        def head_pass(hd, ui):
            s = ui % 2
            W = Wt[s]
            Wq = W[:, 0:2048].rearrange("p (kc n) -> p kc n", kc=8)
            Wk = W[:, 2048:4096].rearrange("p (kc n) -> p kc n", kc=8)
            Wv = W[:, 4096:8192].rearrange("p (kc n) -> p kc n", kc=8)
            Wz = W[:, 8192:12288].rearrange("p (kc n) -> p kc n", kc=8)
            Wo = W[:, 12288:16384].rearrange("p (oc n) -> p oc n", oc=4)
            last = len(chunks) - 1

            def stageG(ci):
                c0, n, ti = chunks[ci]
                par = (hd + ci) % 2
                P.op("pe", lambda e: e.matmul(pgate[0:n, :], lhsT=glrT[0:34, c0:c0 + n], rhs=wg2[0:34, hd * 256:(hd + 1) * 256],
                                              start=True, stop=True), [b_glrT[ti], b_wg2, b_wg2b], [b3a])
                P.op("act", lambda e: e.activation(out=la[0:n, :], in_=pgate[0:n, :], func=AF.Exp, scale=-1.0), [b3a], [b_la])
                P.op("act", lambda e: e.activation(out=la[0:n, :], in_=la[0:n, :], func=AF.Ln, bias=1.0), [b_la], [b_la])
                P.op("dve", lambda e: e.tensor_copy(out=lhi[0:n, :], in_=la[0:n, :]), [b_la], [b_lh])
                P.op("dve", lambda e: e.tensor_tensor(out=llo[0:n, :], in0=la[0:n, :], in1=lhi[0:n, :], op=ALU.subtract),
                     [b_la, b_lh], [b_lh])

            def stageG2(ci):
                c0, n, ti = chunks[ci]
                par = (hd + ci) % 2

                def cum(e):
                    ins = None
                    for dc in range(2):
                        e.matmul(pcum[:, dc, 0:n], lhsT=lhi[0:n, dc * 128:(dc + 1) * 128], rhs=tri[0:n, 0:n], start=True, stop=False)
                        ins = e.matmul(pcum[:, dc, 0:n], lhsT=llo[0:n, dc * 128:(dc + 1) * 128], rhs=tri[0:n, 0:n], start=False, stop=True)
                    return ins
                P.op("pe", cum, [b_lh, b_const], [b3b])
                P.op("act", lambda e: e.activation(out=Eq[par][:, :, 0:n], in_=pcum[:, :, 0:n], func=AF.Exp, scale=-1.0 / 16),
                     [b3b], [bE[par]])
                P.op("act", lambda e: e.activation(out=Ek[par][:, :, 0:n], in_=pcum[:, :, 0:n], func=AF.Exp, scale=1.0 / 16),
                     [b3b], [b_Ek])

            def stageA(ci, mid=None, afterqk=None):
                c0, n, ti = chunks[ci]
                par = (hd + ci) % 2

                def mqk(e):
                    ins = None
                    for idx, Wm in ((0, Wq), (1, Wk)):
                        for dc in range(2):
                            for kc in range(8):
                                ins = e.matmul(pq[:, idx * 2 + dc, 0:n], lhsT=Wm[:, kc, dc * 128:(dc + 1) * 128],
                                               rhs=hnT[:, kc, c0:c0 + n], start=(kc == 0), stop=(kc == 7))
                    return ins
                P.op("pe", mqk, [hnTb[ti], Wb[s][0]], PB(0))
                P.op("dve", lambda e: e.scalar_tensor_tensor(out=qe[par][:, :, 0:n], in0=pq[:, 0:2, 0:n], scalar=0.0625,
                                                             in1=Eq[par][:, :, 0:n], op0=ALU.mult, op1=ALU.mult),
                     PB(0) + [bE[par]], [bQ[par]])
                P.op("dve", lambda e: e.tensor_tensor(out=ke[par][:, :, 0:n], in0=pq[:, 2:4, 0:n], in1=Ek[par][:, :, 0:n], op=ALU.mult),
                     PB(0) + [b_Ek], [bQ[par]])
                if afterqk is not None:
                    afterqk()


                def mv(e):
                    ins = None
                    for kc in range(8):
                        ins = e.matmul(pv[0:n, :], lhsT=hnT[:, kc, c0:c0 + n], rhs=Wv[:, kc, :], start=(kc == 0), stop=(kc == 7))
                    return ins
                P.op("pe", mv, [hnTb[ti], Wb[s][0]], PB(2))
                P.op("act", lambda e: e.activation(out=vb[par][0:n, :], in_=pv[0:n, :], func=AF.Copy), PB(2), [bV[par]])
                for dc in range(2):
                    P.op("act", lambda e, dc=dc: e.activation(out=kdT[par][:, dc, 0:n], in_=ke[par][:, dc, 0:n], func=AF.Copy,
                                                              scale=Eq[par][:, dc, n - 1:n]), [bQ[par], bE[par]], [bQk[par]])
                if mid is not None:
                    mid()

                def mz(e):
                    ins = None
                    for ec in range(4):
                        for kc in range(8):
                            ins = e.matmul(pz[:, ec, 0:n], lhsT=Wz[:, kc, ec * 128:(ec + 1) * 128], rhs=hnT[:, kc, c0:c0 + n],
                                           start=(kc == 0), stop=(kc == 7))
                    return ins
                P.op("pe", mz, [hnTb[ti], Wb[s][1]], PB(1))

            def stageZ(ci):
                c0, n, ti = chunks[ci]
                par = (hd + ci) % 2
                sg = sgs[par]
                P.op("act", lambda e: e.activation(out=sg[:, :, 0:n], in_=pz[:, :, 0:n], func=AF.Exp, scale=-1.0), PB(1), [bsg[par]])
                P.op("act", lambda e: e.activation(out=sg[:, :, 0:n], in_=sg[:, :, 0:n], func=AF.Ln, bias=1.0), [bsg[par]], [bsg[par]])
                P.op("act", lambda e: e.activation(out=sg[:, :, 0:n], in_=sg[:, :, 0:n], func=AF.Exp, scale=-1.0), [bsg[par]], [bsg[par]])

            def stageC(ci):
                c0, n, ti = chunks[ci]
                par = (hd + ci) % 2

                def matt(e):
                    ins = None
                    for dc in range(2):
                        ins = e.matmul(patt[0:n, 0:n], lhsT=ke[par][:, dc, 0:n], rhs=qe[par][:, dc, 0:n], start=(dc == 0), stop=(dc == 1))
                    return ins
                P.op("pe", matt, [bQ[par]], [b4a])
                P.op("dve", lambda e: e.tensor_tensor(out=attm[par][0:n, 0:n], in0=patt[0:n, 0:n], in1=tri[0:n, 0:n], op=ALU.mult),
                     [b4a, b_const], [bA[par]])

                def mkd(e):
                    ins = None
                    for dc in range(2):
                        ins = e.transpose(out=pkdT[0:n, dc * 128:(dc + 1) * 128], in_=kdT[par][:, dc, 0:n], identity=ident[:, :])
                    return ins
                P.op("pe", mkd, [bQk[par], b_const], [b4b])
                P.op("act", lambda e: e.activation(out=kd[par][0:n, :], in_=pkdT[0:n, :], func=AF.Copy), [b4b], [bK[par]])

            def stageD(ci):
                c0, n, ti = chunks[ci]
                par = (hd + ci) % 2

                def mo(e):
                    ins = None
                    for ec in range(4):
                        ins = e.matmul(po[:, ec, 0:n], lhsT=vb[par][0:n, ec * 128:(ec + 1) * 128], rhs=attm[par][0:n, 0:n],
                                       start=True, stop=(ci == 0))
                        if ci > 0:
                            for dc in range(2):
                                ins = e.matmul(po[:, ec, 0:n], lhsT=Sb[:, dc, ec * 128:(ec + 1) * 128], rhs=qe[par][:, dc, 0:n],
                                               start=False, stop=(dc == 1))
                    return ins
                P.op("pe", mo, [bV[par], bA[par], bQ[par]] + ([b_Sb] if ci > 0 else []), PB(5))
                P.op("act", lambda e: e.activation(out=o2[:, :, 0:n], in_=po[:, :, 0:n], func=AF.Square), PB(5), [b_o2])
                sg = sgs[par]
                P.op("dve", lambda e: e.tensor_tensor(out=sg[:, :, 0:n], in0=pz[:, :, 0:n], in1=sg[:, :, 0:n], op=ALU.mult),
                     PB(1) + [bsg[par]], [bsg[par]])
                for ec in range(4):
                    P.op("dve", lambda e, ec=ec: e.scalar_tensor_tensor(out=yT[par][:, ec, 0:n], in0=po[:, ec, 0:n],
                                                                        scalar=hnorm[:, hd * 4 + ec:hd * 4 + ec + 1], in1=sg[:, ec, 0:n],
                                                                        op0=ALU.mult, op1=ALU.mult),
                         PB(5) + [bsg[par], b_vec], [bYt[par]])

            def stageS(ci):
                c0, n, ti = chunks[ci]
                par = (hd + ci) % 2
                if ci < last:
                    for dc in range(2):
                        P.op("pe", lambda e, dc=dc: e.matmul(pS[dc][:, :], lhsT=kd[par][0:n, dc * 128:(dc + 1) * 128], rhs=vb[par][0:n, :],
                                                             start=True, stop=True), [bK[par], bV[par]], PB(6 + dc))
                    for dc in range(2):
                        if ci == 0:
                            P.op("dve", lambda e, dc=dc: e.tensor_copy(out=S[:, dc, :], in_=pS[dc][:, :]), PB(6 + dc), [b_S])
                        else:
                            P.op("dve", lambda e, dc=dc: e.scalar_tensor_tensor(out=S[:, dc, :], in0=S[:, dc, :],
                                                                                scalar=Eq[par][:, dc, n - 1:n], in1=pS[dc][:, :],
                                                                                op0=ALU.mult, op1=ALU.add),
                                 PB(6 + dc) + [bE[par], b_S], [b_S])

            def stageD2(ci):
                if ci < last:
                    P.op("act", lambda e: e.activation(out=Sb[:, :, :], in_=S[:, :, :], func=AF.Copy), [b_S], [b_Sb])

            def stageF(ci):
                c0, n, ti = chunks[ci]
                par = (hd + ci) % 2
                sg = sgs[par]

                def mss(e):
                    ins = None
                    for ec in range(4):
                        ins = e.matmul(pss[0:n, 0:1], lhsT=o2[:, ec, 0:n], rhs=ones[:, 0:1], start=(ec == 0), stop=(ec == 3))
                    return ins
                P.op("pe", mss, [b_o2, b_const], PB(5))
                P.op("act", lambda e: e.activation(out=rtmp[0:n, 0:1], in_=pss[0:n, 0:1], func=AF.Ln, scale=1.0 / 512, bias=epsc[0:n, 0:1]),
                     PB(5) + [b_const], [b_rs])
                P.op("act", lambda e: e.activation(out=rscol[par][0:n, 0:1], in_=rtmp[0:n, 0:1], func=AF.Exp, scale=-0.5),
                     [b_rs], [brc[par]])

            def stageE(ci):
                c0, n, ti = chunks[ci]
                par = (hd + ci) % 2
                for half in range(2):
                    pout = pS[half]

                    def mout(e, half=half, pout=pout):
                        ins = None
                        for ec in range(4):
                            ins = e.matmul(pout[0:n, :], lhsT=yT[par][:, ec, 0:n], rhs=Wo[:, ec, half * 512:(half + 1) * 512],
                                           start=(ec == 0), stop=(ec == 3))
                        return ins
                    P.op("pe", mout, [bYt[par], Wb[s][2]], PB(6 + half))
                    P.op("dve", lambda e, half=half, pout=pout: e.scalar_tensor_tensor(
                        out=h[0:n, ti, half * 512:(half + 1) * 512], in0=pout[0:n, :], scalar=rscol[par][0:n, 0:1],
                        in1=h[0:n, ti, half * 512:(half + 1) * 512], op0=ALU.mult, op1=ALU.add),
                        PB(6 + half) + [hb[ti], brc[par]], [hb[ti]])

            return dict(G=stageG, G2=stageG2, A=stageA, C=stageC, Z=stageZ, D=stageD, S=stageS, D2=stageD2, F=stageF,
                        E=stageE, ui=ui)

        heads = [head_pass(hd, ui0 + hd) for hd in range(4)]
        nch = len(chunks)
        steps = [(hd, ci) for hd in range(4) for ci in range(nch)]
        NS = len(steps)

        def call(name, k, **kw):
            hd, ci = steps[k]
            heads[hd][name](ci, **kw)

        def final_tile(i):
            P.op("act", lambda e: e.activation(out=hnb[:, :], in_=h[:, i, :], func=AF.Square, accum_out=ssb[:, i:i + 1]),
                 [hb[i]], [b_hnb, b_stat[i]])
            P.op("act", lambda e: e.activation(out=lnv[:, i:i + 1], in_=ssb[:, i:i + 1], func=AF.Ln, scale=1.0 / D,
                                               bias=epsc[:, 0:1]), [b_stat[i], b_const], [b_stat[i]])
            P.op("act", lambda e: e.activation(out=rstd[:, i:i + 1], in_=lnv[:, i:i + 1], func=AF.Exp, scale=-0.5),
                 [b_stat[i]], [b_stat[i]])

        def final_tile_b(i):
            P.op("dve", lambda e: e.scalar_tensor_tensor(out=h[:, i, :], in0=h[:, i, :], scalar=rstd[:, i:i + 1],
                                                         in1=gbc[:, :], op0=ALU.mult, op1=ALU.mult),
                 [hb[i], b_stat[i], b_gbc], [hb[i]])
            P.dma("sp", s_out, [(lambda q: q.dma_start(out=out[(i - 1) * 128:i * 128, :], in_=h[:, i, :]), [hb[i]], [])])

        def done_e(k):
            call("E", k)
            hd, ci = steps[k]
            if ci == nch - 1 and heads[hd]["ui"] + 2 < len(units):
                load_unit(heads[hd]["ui"] + 2)

        def maybe_final(k, part):
            hd, ci = steps[k]
            if fuse_final and hd == 3 and ci >= 1:
                (final_tile if part == 0 else final_tile_b)(ci)
            elif (not fuse_final) and li + 1 < n_layers and hd == 3 and part == 0:
                stats_sq(ci, "act" if ci % 2 == 0 else "dve")

        call("G", 0)
        call("G2", 0)
        for k in range(NS):
            def mid(k=k):
                if k + 1 < NS:
                    call("G", k + 1)
                if k >= 1:
                    call("F", k - 1)
                    call("D2", k - 1)
            def afterqk(k=k):
                call("S", k - 1)
                if k >= 2:
                    maybe_final(k - 2, 1)
            call("A", k, mid=mid, afterqk=afterqk if k >= 1 else None)
            call("C", k)
            call("Z", k)
            if k >= 1:
                done_e(k - 1)
            call("D", k)
            if k >= 1:
                maybe_final(k - 1, 0)
            if k + 1 < NS:
                call("G2", k + 1)
        call("F", NS - 1)
        done_e(NS - 1)
        maybe_final(NS - 2, 1)
        maybe_final(NS - 1, 0)
        maybe_final(NS - 1, 1)
        if (not fuse_final) and li + 1 < n_layers:
            sq_ready[0] = True

    def final():
        if n_layers % 2 == 0:
            P.wait_all("sp", hb[1:])
            return
        P.dma("sp", s_gbc, [(lambda q: q.dma_start(out=gbc[:], in_=final_norm[0:1, :].partition_broadcast(128)), [], [b_gbc])])
        junk = PS[3]
        items = []
        for i in range(1, NT):
            P.op("act", lambda e, i=i: e.activation(out=junk[:, :], in_=h[:, i, :], func=AF.Square, accum_out=ssb[:, i:i + 1]),
                 [hb[i]], PB(6, 7) + [b_stat[i]])
            P.op("act", lambda e, i=i: e.activation(out=lnv[:, i:i + 1], in_=ssb[:, i:i + 1], func=AF.Ln, scale=1.0 / D,
                                                    bias=epsc[:, 0:1]), [b_stat[i], b_const], [b_stat[i]])
            P.op("act", lambda e, i=i: e.activation(out=rstd[:, i:i + 1], in_=lnv[:, i:i + 1], func=AF.Exp, scale=-0.5),
                 [b_stat[i]], [b_stat[i]])
            P.op("dve", lambda e, i=i: e.scalar_tensor_tensor(out=h[:, i, :], in0=h[:, i, :], scalar=rstd[:, i:i + 1],
                                                              in1=gbc[:, :], op0=ALU.mult, op1=ALU.mult),
                 [hb[i], b_stat[i], b_gbc], [hb[i]])
            items.append((lambda q, i=i: q.dma_start(out=out[(i - 1) * 128:i * 128, :], in_=h[:, i, :]), [hb[i]], []))
        P.dma("sp", s_out, items)
        P.wait_all("sp", hb[1:])

    load_unit(0)
    if len(units) > 1:
        load_unit(1)
    for li in range(n_layers):
        if li % 2 == 0:
            pool_layer(li, 4 * li)
        else:
            gla_layer(li, 4 * li)
    final()
    P.emit(nc, E)
    es.close()
    return nc


_NC_CACHE = {}


def prep_inputs(x, meta, final_norm, pool_norm, pool_w_in, pool_w_grp, pool_scale, pool_w_out,
                gla_norm, gla_w_in, gla_w_gate2, gla_b_gate, gla_head_norm, gla_w_out):
    f = lambda a: np.ascontiguousarray(np.asarray(a, dtype=np.float32))
    pool_w_in, pool_w_grp, pool_w_out, gla_w_in, gla_w_out = f(pool_w_in), f(pool_w_grp), f(pool_w_out), f(gla_w_in), f(gla_w_out)
    pool_scale, gla_head_norm = f(pool_scale), f(gla_head_norm)
    t3 = lambda a, k: a.reshape(k, 128, -1).transpose(1, 0, 2).reshape(128, -1)
    wp = np.zeros((16, 128, 16384), np.float32)
    for li in range(4):
        j = li // 2
        for u in range(4):
            W = wp[4 * li + u]
            if li % 2 == 0:
                win = pool_w_in[j]
                uz = np.concatenate([win[:, u * 512:(u + 1) * 512], win[:, DI + u * 512:DI + (u + 1) * 512]], axis=1)
                W[:, 0:8192] = t3(uz, 8)
                W[:, 8192:10240] = t3(pool_w_grp[j, u], 4)
                W[:, 12288:16384] = t3(pool_w_out[j, u * 512:(u + 1) * 512], 4)
            else:
                win = gla_w_in[j]
                W[:, 0:2048] = t3(win[:, u * 256:(u + 1) * 256], 8)
                W[:, 2048:4096] = t3(win[:, 1024 + u * 256:1024 + (u + 1) * 256], 8)
                W[:, 4096:8192] = t3(win[:, 2048 + u * 512:2048 + (u + 1) * 512], 8)
                W[:, 8192:12288] = t3(win[:, 4096 + u * 512:4096 + (u + 1) * 512], 8)
                W[:, 12288:16384] = t3(gla_w_out[j, u * 512:(u + 1) * 512], 4)
    vecp = np.zeros((128, 64), np.float32)
    for j in range(2):
        vecp[:, 16 * j:16 * j + 16] = pool_scale[j].reshape(16, 128).T
        vecp[:, 32 + 16 * j:48 + 16 * j] = gla_head_norm[j].reshape(16, 128).T
    wglrp = np.stack([t3(gla_w_in[j][:, 6144:6160], 8) for j in range(2)], axis=0)
    shared = {
        "meta": f(meta), "final_norm": f(final_norm).reshape(1, D), "pool_norm": f(pool_norm), "gla_norm": f(gla_norm),
        "gla_w_gate2": f(gla_w_gate2), "gla_b_gate": f(gla_b_gate), "cst": host_consts(),
        "wpack": wp, "vecp": vecp, "wglrp": np.ascontiguousarray(wglrp),
    }
    xs = f(x)
    xr = [np.ascontiguousarray(xs[b].reshape(16, 128, D).transpose(1, 0, 2).reshape(128, 16 * D)) for b in range(xs.shape[0])]
    return shared, xr


def kernel(x, meta, final_norm, pool_norm, pool_w_in, pool_w_grp, pool_scale, pool_w_out,
           gla_norm, gla_w_in, gla_w_gate2, gla_b_gate, gla_head_norm, gla_w_out, _n_layers=4, _cores=8):
    shared, xr = prep_inputs(x, meta, final_norm, pool_norm, pool_w_in, pool_w_grp, pool_scale, pool_w_out,
                             gla_norm, gla_w_in, gla_w_gate2, gla_b_gate, gla_head_norm, gla_w_out)
    if _n_layers not in _NC_CACHE:
        _NC_CACHE[_n_layers] = build(_n_layers)
    nc = _NC_CACHE[_n_layers]
    in_maps = [dict(shared, x=xr[b]) for b in range(_cores)]
    res = run_bass_kernel_spmd(nc, in_maps, core_ids=list(range(_cores)))
    return np.stack([res.results[b]["out"] for b in range(_cores)], axis=0)
```

```python
from contextlib import ExitStack
import numpy as np
import concourse.bass as bass
import concourse.mybir as mybir
from concourse.bass_utils import run_bass_kernel_spmd

F32 = mybir.dt.float32
BF16 = mybir.dt.bfloat16
AF = mybir.ActivationFunctionType
ALU = mybir.AluOpType

D = 1024
SEQ = 2048
NMETA = 16
L = SEQ + NMETA
NT = 17
DI = 2048
WINS = (2, 4, 8, 16)
GLA_IN = 6160
EPS = 1e-6
ENGS = ("pe", "act", "dve", "pool", "sp")


class Buf:
    __slots__ = ("name", "lw", "rd", "psum")

    def __init__(self, name, psum=False):
        self.name = name
        self.lw = None
        self.rd = {}
        self.psum = psum


class DSem:
    def __init__(self, name):
        self.name = name
        self.count = 0
        self.handle = None


class Prog:
    def __init__(self):
        self.q = {e: [] for e in ENGS}
        self.cnt = {e: 0 for e in ENGS}
        self.seen = {e: {} for e in ENGS}
        self.dsems = {}

    def dsem(self, name):
        s = DSem("d_" + name)
        self.dsems[s.name] = s
        return s

    def _deps(self, eng, reads, writes, ignore=None):
        w = {}

        def need(dep):
            if dep is None or dep == ignore:
                return
            k, v = dep
            if self.seen[eng].get(k, 0) >= v:
                return
            if w.get(k, 0) < v:
                w[k] = v

        for b in reads:
            need(b.lw)
            if b.psum:
                for k, v in b.rd.items():
                    if k != eng:
                        need((k, v))
        for b in writes:
            need(b.lw)
            for k, v in b.rd.items():
                need((k, v))
        for k, v in w.items():
            self.seen[eng][k] = v
        return w

    def op(self, eng, fn, reads=(), writes=()):
        w = self._deps(eng, reads, writes)
        idx = self.cnt[eng] + 1
        self.cnt[eng] = idx
        self.q[eng].append((w, fn, True, None))
        for b in reads:
            b.rd[eng] = idx
        for b in writes:
            b.lw = (eng, idx)
            b.rd = {}

    def dma(self, queue, sem, items):
        pre = {}
        if sem.count > 0 and self.seen[queue].get(sem.name, 0) < sem.count:
            pre[sem.name] = sem.count
            self.seen[queue][sem.name] = sem.count
        final = sem.count + 16 * len(items)
        first = True
        for fn, reads, writes in items:
            w = self._deps(queue, reads, writes, ignore=(sem.name, final))
            if first:
                for k, v in pre.items():
                    if w.get(k, 0) < v:
                        w[k] = v
                first = False
            self.q[queue].append((w, fn, False, sem))
            for b in reads:
                b.rd[sem.name] = final
            for b in writes:
                b.lw = (sem.name, final)
                b.rd = {}
        sem.count = final

    def barrier(self, engs=("pe", "act", "dve")):
        for e in engs:
            w = {}
            for e2 in engs:
                v = self.cnt[e2]
                if v > 0 and self.seen[e].get(e2, 0) < v:
                    w[e2] = v
                    self.seen[e][e2] = v
            if w:
                self.q[e].append((w, None, False, None))

    def wait_all(self, eng, bufs):
        w = self._deps(eng, [], bufs)
        self.q[eng].append((w, None, False, None))

    def emit(self, nc, E):
        esem = {e: E(nc.semaphore("s_" + e)) for e in ENGS}
        for s in self.dsems.values():
            s.handle = E(nc.semaphore(s.name))

        def semof(k):
            return esem[k] if k in esem else self.dsems[k].handle

        block = E(nc.Block())

        def replay(e, h):
            for (w, fn, signal, dsem) in self.q[e]:
                for k, v in w.items():
                    h.wait_ge(semof(k), v)
                if fn is None:
                    continue
                ins = fn(h)
                if dsem is not None:
                    ins.then_inc(dsem.handle, 16)
                elif signal:
                    ins.then_inc(esem[e], 1)

        @block.tensor
        def _(h):
            replay("pe", h)

        @block.scalar
        def _(h):
            replay("act", h)

        @block.vector
        def _(h):
            replay("dve", h)

        @block.gpsimd
        def _(h):
            replay("pool", h)

        @block.sync
        def _(h):
            replay("sp", h)


def host_consts():
    c = np.zeros((128, 512), np.float32)
    c[:, 0:128] = np.eye(128, dtype=np.float32)
    c[:, 128:256] = np.triu(np.ones((128, 128), np.float32))
    inv = np.zeros((4, 4, 16), np.float32)
    for g, w in enumerate(WINS):
        for t in range(16):
            inv[g, :, t] = 1.0 / min(t + 1, w)
    c[:, 256:512] = inv.reshape(1, 256)
    return c


def build(n_layers=4):
    nc = bass.Bass("TRN2", target_bir_lowering=False)
    dt = lambda name, shape: nc.dram_tensor(name, shape, F32, kind="ExternalInput").ap()
    x = dt("x", [128, 16 * D])
    meta = dt("meta", [NMETA, D])
    final_norm = dt("final_norm", [1, D])
    pool_norm = dt("pool_norm", [2, D])
    wpack = dt("wpack", [16, 128, 16384])
    vecp = dt("vecp", [128, 64])
    wglrp = dt("wglrp", [2, 128, 128])
    gla_norm = dt("gla_norm", [2, D])
    gla_w_gate2 = dt("gla_w_gate2", [2, 16, 1024])
    gla_b_gate = dt("gla_b_gate", [2, 1024])
    cst = dt("cst", [128, 512])
    out = nc.dram_tensor("out", [SEQ, D], F32, kind="ExternalOutput").ap()

    P = Prog()
    es = ExitStack()
    E = es.enter_context
    sb = lambda name, shape, dtype: E(nc.sbuf_tensor(name, shape, dtype))

    h = sb("h", [128, NT, D], F32)
    hnT = sb("hnT", [128, 8, L], BF16)
    Wt = [sb("W0", [128, 16384], BF16), sb("W1", [128, 16384], BF16)]
    ident = sb("ident", [128, 128], BF16)
    tri = sb("tri", [128, 128], BF16)
    ones = sb("ones", [128, 128], BF16)
    invcf = sb("invcf", [128, 256], F32)
    pscale = sb("pscale", [128, 16], F32)
    hnorm = sb("hnorm", [128, 16], F32)
    epsc = sb("epsc", [128, 1], F32)
    ssb = sb("ssb", [128, NT], F32)
    lnv = sb("lnv", [128, NT], F32)
    rstd = sb("rstd", [128, NT], F32)
    Wglr = sb("Wglr", [128, 8, 16], BF16)
    wg2 = sb("wg2", [64, 1024], BF16)
    glrT = sb("glrT", [64, L], BF16)
    gbc = sb("gbc", [128, D], F32)
    hnb = sb("hnb", [128, D], BF16)
    invc = invcf[:, :].rearrange("p (g f t) -> p g f t", g=4, f=4)

    ARENA = 29696
    arena = sb("arena", [128, ARENA // 4], F32)
    _off = [0]

    def carve(nbytes, dtype, pat=None, **kw):
        a = _off[0]
        assert a % 4 == 0 and nbytes % 4 == 0
        _off[0] = a + nbytes
        assert _off[0] <= ARENA, _off[0]
        ap = arena[:, a // 4:(a + nbytes) // 4]
        if dtype == BF16:
            ap = ap.bitcast(BF16)
        if pat:
            ap = ap.rearrange(pat, **kw)
        return ap

    PS = [E(nc.psum_tensor("ps%d" % i, [128, 1024], F32)) for i in range(4)]
    bank = lambda i: PS[i // 2][:, (i % 2) * 512:(i % 2) * 512 + 512]
    pbl = [[Buf("bank%d" % i, psum=True)] for i in range(8)]
    b3a = b3b = pbl[3][0]
    b4a = b4c = pbl[4][0]
    b4b = pbl[2][0]
    PB = lambda *idx: [b for i in idx for b in pbl[i]]

    hb = [Buf("h%d" % i) for i in range(NT)]
    hnTb = [Buf("hnT%d" % i) for i in range(NT)]
    Wb = [[Buf("W%d_%d" % (s_, p_)) for p_ in range(3)] for s_ in range(2)]
    Wsem = [[P.dsem("W%d_%d" % (s_, p_)) for p_ in range(3)] for s_ in range(2)]
    b_const = Buf("const")
    b_gbc = Buf("gbc")
    b_hnb = Buf("hnb")
    b_vec = Buf("vec")
    b_wglr = Buf("wglr")
    b_wg2 = Buf("wg2")
    b_wg2b = Buf("wg2b")
    b_glrT = [Buf("glrT%d" % i) for i in range(NT)]
    b_stat = [Buf("stat%d" % i) for i in range(NT)]
    s_const = P.dsem("const")
    s_x = [P.dsem("x%d" % i) for i in range(8)]
    s_meta = P.dsem("meta")
    s_gbc = P.dsem("gbc")
    s_vec = P.dsem("vec")
    s_wglr = P.dsem("wglr")
    s_wg2 = P.dsem("wg2")
    s_wg2b = P.dsem("wg2b")
    s_out = P.dsem("out")
    s_out2 = [P.dsem("outa"), P.dsem("outb")]

    tile_rows = lambda i: NMETA if i == 0 else 128
    tile_col0 = lambda i: 0 if i == 0 else NMETA + (i - 1) * 128

    P.dma("pool", s_const, [(lambda q: q.dma_start(out=arena[:, 0:256], in_=cst[:, 0:256]), [], [b_const]),
                          (lambda q: q.dma_start(out=invcf[:], in_=cst[:, 256:512]), [], [b_const])])
    P.op("dve", lambda e: e.tensor_copy(out=ident[:], in_=arena[:, 0:128]), [b_const], [b_const])
    P.op("dve", lambda e: e.tensor_copy(out=tri[:], in_=arena[:, 128:256]), [b_const], [b_const])
    P.op("dve", lambda e: e.memset(ones[:], 1.0), [], [b_const])
    P.op("dve", lambda e: e.memset(epsc[:], EPS), [], [b_const])
    def load_x():
        P.dma("sp", s_meta, [(lambda q: q.dma_start(out=h[0:NMETA, 0, :], in_=meta[:, :]), [], [hb[0]])])
        xv = x.rearrange("p (t d) -> p t d", t=16)
        for k in range(8):
            P.dma("sp", s_x[k],
                  [(lambda q, k=k: q.dma_start(out=h[:, 1 + 2 * k:3 + 2 * k, :], in_=xv[:, 2 * k:2 * k + 2, :]),
                    [], [hb[1 + 2 * k + r] for r in range(2)])])

    units = []
    for li in range(n_layers):
        for u in range(4):
            units.append((li, u))

    def load_unit(ui):
        li, u = units[ui]
        s = ui % 2
        W = Wt[s]
        j = li // 2
        items = [[], [], []]
        src = wpack[ui]
        rng = [(0, 4096), (4096, 8192)], [(8192, 10240) if li % 2 == 0 else (8192, 12288)], [(12288, 16384)]
        for p_ in range(3):
            for (c0_, c1_) in rng[p_]:
                items[p_].append((lambda q, c0_=c0_, c1_=c1_: q.dma_start(out=W[:, c0_:c1_], in_=src[:, c0_:c1_]), [], [Wb[s][p_]]))
        for p_ in range(3):
            it = items[p_]
            if not it:
                continue
            if ui == 1 or (ui == 0 and p_ >= 1):
                it = [(fn, list(r) + [hb[NT - 1]], w) for fn, r, w in it]
            P.dma("pool", Wsem[s][p_], it)

    x_loaded = [False]
    sq_ready = [False]

    def stats_sq(i, eng):
        n = tile_rows(i)
        c0 = tile_col0(i)
        if i == 0:
            jout, src, jb = hnb[0:n, :], h[0:n, 0, :], [b_hnb]
        else:
            jout, src, jb = hnT[:, :, c0:c0 + 128], h[:, i, :].rearrange("p (k m) -> p k m", k=8), [hnTb[i]]
        if eng == "act":
            P.op("act", lambda e: e.activation(out=jout, in_=src, func=AF.Square, accum_out=ssb[0:n, i:i + 1]),
                 [hb[i]], jb + [b_stat[i]])
        else:
            P.op("dve", lambda e: e.scalar_tensor_tensor(out=jout, in0=src, scalar=1.0, in1=src, op0=ALU.mult, op1=ALU.mult,
                                                         accum_out=ssb[0:n, i:i + 1]), [hb[i]], jb + [b_stat[i]])

    def prepass(gvec_ap, gla):
        P.dma("sp", s_gbc, [(lambda q: q.dma_start(out=gbc[:], in_=gvec_ap.partition_broadcast(128)), [], [b_gbc])])
        if not x_loaded[0]:
            load_x()
            x_loaded[0] = True
        P.barrier()
        hnbs = [hnb, arena[:, 0:512].bitcast(BF16)]
        b_hnbs = [b_hnb, Buf("hnb2")]
        junk = PS[3]

        junk2 = PS[2]

        have_sq = sq_ready[0]
        sq_ready[0] = False

        if have_sq:
            for (r_, c0_, c1_) in ((NMETA, 0, 1), (128, 1, NT)):
                bs_ = [b_stat[t_] for t_ in range(c0_, c1_)]
                P.op("act", lambda e, r_=r_, c0_=c0_, c1_=c1_: e.activation(out=lnv[0:r_, c0_:c1_], in_=ssb[0:r_, c0_:c1_], func=AF.Ln,
                                                                         scale=1.0 / D, bias=epsc[0:r_, 0:1]), bs_ + [b_const], bs_)
                P.op("act", lambda e, r_=r_, c0_=c0_, c1_=c1_: e.activation(out=rstd[0:r_, c0_:c1_], in_=lnv[0:r_, c0_:c1_], func=AF.Exp,
                                                                         scale=-0.5), bs_, bs_)

        def stats(i):
            n = tile_rows(i)
            if have_sq:
                return
            elif i % 2 == 0:
                P.op("act", lambda e: e.activation(out=junk[0:n, :], in_=h[0:n, i, :], func=AF.Square, accum_out=ssb[0:n, i:i + 1]),
                     [hb[i]], PB(6, 7) + [b_stat[i]])
            else:
                P.op("dve", lambda e: e.scalar_tensor_tensor(out=junk2[0:n, :], in0=h[0:n, i, :], scalar=1.0, in1=h[0:n, i, :],
                                                             op0=ALU.mult, op1=ALU.mult, accum_out=ssb[0:n, i:i + 1]),
                     [hb[i]], PB(4, 5) + [b_stat[i]])
            P.op("act", lambda e: e.activation(out=lnv[0:n, i:i + 1], in_=ssb[0:n, i:i + 1], func=AF.Ln, scale=1.0 / D,
                                               bias=epsc[0:n, 0:1]), [b_stat[i], b_const], [b_stat[i]])
            P.op("act", lambda e: e.activation(out=rstd[0:n, i:i + 1], in_=lnv[0:n, i:i + 1], func=AF.Exp, scale=-0.5),
                 [b_stat[i]], [b_stat[i]])

        def body(i):
            n = tile_rows(i)
            c0 = tile_col0(i)
            hb_ = hnbs[i % 2]
            bb_ = b_hnbs[i % 2]
            P.op("dve", lambda e: e.scalar_tensor_tensor(out=hb_[0:n, :], in0=h[0:n, i, :], scalar=rstd[0:n, i:i + 1],
                                                         in1=gbc[0:n, :], op0=ALU.mult, op1=ALU.mult),
                 [hb[i], b_stat[i], b_gbc], [bb_])
            pb = i % 2
            pT = bank(pb).bitcast(BF16).rearrange("p (kc n) -> p kc n", kc=8)

            def tr(e):
                ins = None
                for kc in range(8):
                    ins = e.transpose(out=pT[:, kc, 0:n], in_=hb_[0:n, kc * 128:(kc + 1) * 128], identity=ident[0:n, 0:n])
                return ins
            P.op("pe", tr, [bb_, b_const], PB(pb))
            return pT

        def evac(i, pT):
            n = tile_rows(i)
            c0 = tile_col0(i)
            pb = i % 2
            P.op("act", lambda e: e.activation(out=hnT[:, :, c0:c0 + n], in_=pT[:, :, 0:n], func=AF.Copy), PB(pb), [hnTb[i]])

        def glr_mm(i):
            n = tile_rows(i)
            c0 = tile_col0(i)
            pb = i % 2
            pg = bank(2 + pb)

            def glr(e):
                ins = None
                for kc in range(8):
                    ins = e.matmul(pg[0:16, 0:n], lhsT=Wglr[:, kc, :], rhs=hnT[:, kc, c0:c0 + n], start=(kc == 0), stop=(kc == 7))
                return ins
            P.op("pe", glr, [hnTb[i], b_wglr], PB(2 + pb))

        def glr_evac(i):
            n = tile_rows(i)
            c0 = tile_col0(i)
            pg = bank(2 + i % 2)
            P.op("act", lambda e: e.activation(out=glrT[0:16, c0:c0 + n], in_=pg[0:16, 0:n], func=AF.Copy),
                 PB(2 + i % 2), [b_glrT[i]])

        stats(0)
        stats(1)
        for i in range(NT):
            pT = body(i)
            if i + 2 < NT:
                stats(i + 2)
            evac(i, pT)
            if gla and i >= 1:
                glr_mm(i - 1)
            if gla and i >= 2:
                glr_evac(i - 2)
        if gla:
            glr_mm(NT - 1)
            glr_evac(NT - 2)
            glr_evac(NT - 1)

    def pool_layer(li, ui0):
        j = li // 2
        prepass(pool_norm[j:j + 1, :], False)
        P.dma("sp", s_vec, [(lambda q: q.dma_start(out=pscale[:], in_=vecp[:, 16 * j:16 * j + 16]), [], [b_vec])])
        P.barrier()
        _off[0] = 0
        NB = 256
        U = [carve(4 * 272 * 4, F32, "p (f m) -> p f m", f=4) for _ in range(2)]
        SA = carve(4 * 272 * 4, F32, "p (f m) -> p f m", f=4)
        SB = carve(4 * 272 * 4, F32, "p (f m) -> p f m", f=4)
        T = [carve(4 * NB * 2, BF16, "p (f m) -> p f m", f=4) for _ in range(2)]
        SZ = [carve(4 * NB * 2, BF16, "p (f m) -> p f m", f=4) for _ in range(2)]
        Y = [carve(4 * NB * 2, BF16, "p (f m) -> p f m", f=4) for _ in range(2)]
        bU = [Buf("U0"), Buf("U1")]
        bS = Buf("SAB")
        bT = [Buf("T0"), Buf("T1")]
        bSZ = [Buf("SZ0"), Buf("SZ1")]
        bY = [Buf("Y0"), Buf("Y1")]
        blocks = [(0, NMETA, [0])] + [(NMETA + NB * b, NB, [1 + 2 * b, 2 + 2 * b]) for b in range(SEQ // NB)]
        pu = PS[0].rearrange("p (f m) -> p f m", f=4)
        pz = PS[1].rearrange("p (f m) -> p f m", f=4)
        py = PS[0].rearrange("p (f m) -> p f m", f=4)

        def group_pass(g, ui):
            s = ui % 2
            W = Wt[s]
            w = WINS[g]
            Wuz = W[:, 0:8192].rearrange("p (kc n) -> p kc n", kc=8)
            Wg = W[:, 8192:10240].rearrange("p (fc n) -> p fc n", fc=4)
            Wo = W[:, 12288:16384].rearrange("p (oc n) -> p oc n", oc=4)

            fins = {}

            def stageA(bi):
                c0, n, tiles = blocks[bi]
                par = (g + bi) % 2
                Ub = U[par]
                rd_h = [hnTb[t] for t in tiles]

                def mm(e, off, dst):
                    ins = None
                    for fc in range(4):
                        for kc in range(8):
                            ins = e.matmul(dst[:, fc, 0:n], lhsT=Wuz[:, kc, off + fc * 128:off + (fc + 1) * 128],
                                           rhs=hnT[:, kc, c0:c0 + n], start=(kc == 0), stop=(kc == 7))
                    return ins
                P.op("pe", lambda e: mm(e, 0, pu), rd_h + [Wb[s][0]], PB(0, 1))
                if bi == 0:
                    P.op("act", lambda e: e.memzero(Ub[:, :, 0:16]), [], [bU[par]])
                else:
                    pn = blocks[bi - 1][1]
                    Up = U[1 - par]
                    P.op("act", lambda e: e.activation(out=Ub[:, :, 0:16], in_=Up[:, :, pn:pn + 16], func=AF.Copy),
                         [bU[1 - par]], [bU[par]])
                P.op("act", lambda e: e.activation(out=Ub[:, :, 16:16 + n], in_=pu[:, :, 0:n], func=AF.Copy),
                     PB(0, 1), [bU[par]])
                P.op("pe", lambda e: mm(e, 512, pz), rd_h + [Wb[s][0]], PB(2, 3))
                P.op("act", lambda e: e.activation(out=SZ[par][:, :, 0:n], in_=pz[:, :, 0:n], func=AF.Silu),
                     PB(2, 3), [bSZ[par]])
                e_ = 16 + n
                add = lambda dst, a, b: (lambda e: e.tensor_tensor(out=dst, in0=a, in1=b, op=ALU.add))
                AE = "pool"
                if w == 2:
                    P.op(AE, add(SA[:, :, 16:e_], Ub[:, :, 16:e_], Ub[:, :, 15:e_ - 1]), [bU[par]], [bS])
                    fin = SA
                elif w == 4:
                    P.op(AE, add(SA[:, :, 14:e_], Ub[:, :, 14:e_], Ub[:, :, 13:e_ - 1]), [bU[par]], [bS])
                    P.op(AE, add(SB[:, :, 16:e_], SA[:, :, 16:e_], SA[:, :, 14:e_ - 2]), [bS], [bS])
                    fin = SB
                elif w == 8:
                    P.op(AE, add(SA[:, :, 10:e_], Ub[:, :, 10:e_], Ub[:, :, 9:e_ - 1]), [bU[par]], [bS])
                    P.op(AE, add(SB[:, :, 12:e_], SA[:, :, 12:e_], SA[:, :, 10:e_ - 2]), [bS], [bS])
                    P.op(AE, add(SA[:, :, 16:e_], SB[:, :, 16:e_], SB[:, :, 12:e_ - 4]), [bS], [bS])
                    fin = SA
                else:
                    P.op(AE, add(SA[:, :, 2:e_], Ub[:, :, 2:e_], Ub[:, :, 1:e_ - 1]), [bU[par]], [bS])
                    P.op(AE, add(SB[:, :, 4:e_], SA[:, :, 4:e_], SA[:, :, 2:e_ - 2]), [bS], [bS])
                    P.op(AE, add(SA[:, :, 8:e_], SB[:, :, 8:e_], SB[:, :, 4:e_ - 4]), [bS], [bS])
                    P.op(AE, add(SB[:, :, 16:e_], SA[:, :, 16:e_], SA[:, :, 8:e_ - 8]), [bS], [bS])
                    fin = SB
                fins[bi] = fin

            def stageA2(bi):
                c0, n, tiles = blocks[bi]
                par = (g + bi) % 2
                Ub = U[par]
                e_ = 16 + n
                fin = fins[bi]
                if bi == 0:
                    P.op("dve", lambda e: e.tensor_tensor(out=fin[:, :, 16:e_], in0=fin[:, :, 16:e_], in1=invc[:, g, :, :],
                                                          op=ALU.mult), [bS, b_const], [bS])
                    P.op("dve", lambda e: e.tensor_tensor(out=T[par][:, :, 0:n], in0=fin[:, :, 16:e_], in1=Ub[:, :, 16:e_],
                                                          op=ALU.subtract), [bS, bU[par]], [bT[par]])
                else:
                    for fc in range(4):
                        P.op("dve", lambda e, fc=fc: e.scalar_tensor_tensor(out=T[par][:, fc, 0:n], in0=fin[:, fc, 16:e_],
                                                                            scalar=1.0 / w, in1=Ub[:, fc, 16:e_],
                                                                            op0=ALU.mult, op1=ALU.subtract),
                             [bS, bU[par]], [bT[par]])

            def stageB(bi):
                c0, n, tiles = blocks[bi]
                par = (g + bi) % 2

                def mmg(e):
                    ins = None
                    for oc in range(4):
                        for fc in range(4):
                            ins = e.matmul(py[:, oc, 0:n], lhsT=Wg[:, fc, oc * 128:(oc + 1) * 128], rhs=T[par][:, fc, 0:n],
                                           start=(fc == 0), stop=(fc == 3))
                    return ins
                P.op("pe", mmg, [bT[par], Wb[s][1]], PB(0, 1))
                for oc in range(4):
                    P.op("dve", lambda e, oc=oc: e.scalar_tensor_tensor(out=Y[par][:, oc, 0:n], in0=py[:, oc, 0:n],
                                                                        scalar=pscale[:, g * 4 + oc:g * 4 + oc + 1],
                                                                        in1=SZ[par][:, oc, 0:n], op0=ALU.mult, op1=ALU.mult),
                         PB(0, 1) + [bSZ[par], b_vec], [bY[par]])

            def stageB2(bi):
                c0, n, tiles = blocks[bi]
                par = (g + bi) % 2
                for k, ti in enumerate(tiles):
                    nt = tile_rows(ti)
                    cc = k * 128
                    for half in range(2):
                        pbi = 4 + 2 * (k % 2) + half
                        po = bank(pbi)

                        def mmo(e, half=half, cc=cc, nt=nt, po=po):
                            ins = None
                            for oc in range(4):
                                ins = e.matmul(po[0:nt, :], lhsT=Y[par][:, oc, cc:cc + nt], rhs=Wo[:, oc, half * 512:(half + 1) * 512],
                                               start=(oc == 0), stop=(oc == 3))
                            return ins
                        P.op("pe", mmo, [bY[par], Wb[s][2]], PB(pbi))
                        P.op("dve", lambda e, half=half, nt=nt, ti=ti, po=po: e.tensor_tensor(
                            out=h[0:nt, ti, half * 512:(half + 1) * 512], in0=h[0:nt, ti, half * 512:(half + 1) * 512],
                            in1=po[0:nt, :], op=ALU.add), PB(pbi) + [hb[ti]], [hb[ti]])

            return dict(A=stageA, A2=stageA2, B=stageB, B2=stageB2, ui=ui)

        groups = [group_pass(g, ui0 + g) for g in range(4)]
        nb = len(blocks)
        steps = [(g, bi) for g in range(4) for bi in range(nb)]
        NS = len(steps)

        def call(name, k):
            g, bi = steps[k]
            groups[g][name](bi)

        def done_b2(k):
            call("B2", k)
            g, bi = steps[k]
            if g == 3 and li + 1 < n_layers:
                for t_ in blocks[bi][2]:
                    stats_sq(t_, "act")
            if bi == nb - 1 and groups[g]["ui"] + 2 < len(units):
                load_unit(groups[g]["ui"] + 2)

        for k in range(NS):
            call("A", k)
            if k >= 1:
                call("B", k - 1)
            if k >= 2:
                done_b2(k - 2)
            call("A2", k)
            if k == 3 and li + 1 < n_layers:
                gla_preload((li + 1) // 2)
        call("B", NS - 1)
        done_b2(NS - 2)
        done_b2(NS - 1)
        if li + 1 < n_layers:
            sq_ready[0] = True

    def gla_preload(j):
        P.dma("sp", s_vec, [(lambda q: q.dma_start(out=hnorm[:], in_=vecp[:, 32 + 16 * j:48 + 16 * j]), [], [b_vec])])
        wglr_src = wglrp[j].rearrange("p (kc n) -> p kc n", kc=8)
        P.dma("pool", s_wglr, [(lambda q: q.dma_start(out=Wglr[:, :, :], in_=wglr_src), [], [b_wglr])])
        P.op("act", lambda e: e.memzero(wg2[0:32, :]), [], [b_wg2])
        P.op("act", lambda e: e.memzero(wg2[32:64, :]), [], [b_wg2b])
        P.dma("pool", s_wg2, [(lambda q: q.dma_start(out=wg2[0:16, :], in_=gla_w_gate2[j]), [], [b_wg2])])
        P.dma("sp", s_gbc, [(lambda q: q.dma_start(out=gbc[32:33, :], in_=gla_b_gate[j:j + 1, :]), [], [b_gbc])])
        P.op("dve", lambda e: e.tensor_copy(out=wg2[32:33, :], in_=gbc[32:33, :]), [b_gbc], [b_wg2b])
        P.op("dve", lambda e: e.tensor_tensor(out=hnb[32:33, :], in0=gbc[32:33, :], in1=wg2[32:33, :], op=ALU.subtract),
             [b_gbc, b_wg2b], [b_hnb])
        P.dma("sp", s_wg2b, [(lambda q: q.dma_start(out=wg2[33:34, :], in_=hnb[32:33, :]), [b_hnb], [b_wg2b])])
        P.op("act", lambda e: e.memzero(glrT[0:32, :]), [], list(b_glrT))
        P.op("act", lambda e: e.memzero(glrT[32:64, :]), [], list(b_glrT))
        P.op("act", lambda e: e.activation(out=glrT[32:64, :], in_=glrT[32:64, :], func=AF.Copy, scale=0.0, bias=1.0),
             list(b_glrT), list(b_glrT))

    def gla_layer(li, ui0):
        j = li // 2
        prepass(gla_norm[j:j + 1, :], True)
        fuse_final = (li == n_layers - 1)
        if fuse_final:
            P.dma("sp", s_gbc, [(lambda q: q.dma_start(out=gbc[:], in_=final_norm[0:1, :].partition_broadcast(128)), [], [b_gbc])])
        P.barrier()
        _off[0] = 0
        v3 = dict(pat="p (f m) -> p f m")
        Eq = [carve(1024, F32, f=2, **v3) for _ in range(2)]
        Ek1 = carve(1024, F32, f=2, **v3)
        Ek = [Ek1, Ek1]
        la = carve(1024, F32)
        lhi = carve(512, BF16)
        llo = carve(512, BF16)
        rscol = [carve(16, F32) for _ in range(2)]
        qe = [carve(512, BF16, f=2, **v3) for _ in range(2)]
        ke = [carve(512, BF16, f=2, **v3) for _ in range(2)]
        kdT = [carve(512, BF16, f=2, **v3) for _ in range(2)]
        attm = [carve(256, BF16) for _ in range(2)]
        kd = [carve(512, BF16) for _ in range(2)]
        vb = [carve(1024, BF16) for _ in range(2)]
        o2 = carve(1024, BF16, f=4, **v3)
        sgs = [carve(2048, F32, f=4, **v3) for _ in range(2)]
        yT = [carve(1024, BF16, f=4, **v3) for _ in range(2)]
        rtmp = carve(16, F32)
        S = carve(4096, F32, f=2, **v3)
        Sb = carve(2048, BF16, f=2, **v3)
        bE = [Buf("E0"), Buf("E1")]
        b_Ek = Buf("Ek")
        bsg = [Buf("sg0"), Buf("sg1")]
        b_la, b_lh, b_nb = Buf("la"), Buf("lh"), Buf("nb")
        bQ = [Buf("Q0"), Buf("Q1")]
        bQk = [Buf("Qk0"), Buf("Qk1")]
        brc = [Buf("rc0"), Buf("rc1")]
        bA = [Buf("A0"), Buf("A1")]
        bK = [Buf("K0"), Buf("K1")]
        bV = [Buf("V0"), Buf("V1")]
        b_oT, b_o2, b_sg, b_rs, b_S, b_Sb = Buf("oT"), Buf("o2"), Buf("sg"), Buf("rs"), Buf("S"), Buf("Sb")
        bYt = [Buf("yT0"), Buf("yT1")]
        chunks = [(tile_col0(i), tile_rows(i), i) for i in range(NT)]
        pq = bank(0).rearrange("p (f m) -> p f m", f=4)
        pz = bank(1).rearrange("p (f m) -> p f m", f=4)
        pv = bank(2)
        pgate = bank(3)[:, 0:256]
        pcum = bank(3)[:, 256:512].rearrange("p (f m) -> p f m", f=2)
        patt = bank(4)[:, 0:128]
        pkdT = bank(2)[:, 0:128].bitcast(BF16)
        pss = bank(5)[:, 0:128]
        po = bank(5).rearrange("p (f m) -> p f m", f=4)
        pS = [bank(6), bank(7)]

        def head_pass(hd, ui):
            s = ui % 2
            W = Wt[s]
            Wq = W[:, 0:2048].rearrange("p (kc n) -> p kc n", kc=8)
            Wk = W[:, 2048:4096].rearrange("p (kc n) -> p kc n", kc=8)
            Wv = W[:, 4096:8192].rearrange("p (kc n) -> p kc n", kc=8)
            Wz = W[:, 8192:12288].rearrange("p (kc n) -> p kc n", kc=8)
            Wo = W[:, 12288:16384].rearrange("p (oc n) -> p oc n", oc=4)
            last = len(chunks) - 1

            def stageG(ci):
                c0, n, ti = chunks[ci]
                par = (hd + ci) % 2
                P.op("pe", lambda e: e.matmul(pgate[0:n, :], lhsT=glrT[0:34, c0:c0 + n], rhs=wg2[0:34, hd * 256:(hd + 1) * 256],
                                              start=True, stop=True), [b_glrT[ti], b_wg2, b_wg2b], [b3a])
                P.op("act", lambda e: e.activation(out=la[0:n, :], in_=pgate[0:n, :], func=AF.Exp, scale=-1.0), [b3a], [b_la])
                P.op("act", lambda e: e.activation(out=la[0:n, :], in_=la[0:n, :], func=AF.Ln, bias=1.0), [b_la], [b_la])
                P.op("dve", lambda e: e.tensor_copy(out=lhi[0:n, :], in_=la[0:n, :]), [b_la], [b_lh])
                P.op("dve", lambda e: e.tensor_tensor(out=llo[0:n, :], in0=la[0:n, :], in1=lhi[0:n, :], op=ALU.subtract),
                     [b_la, b_lh], [b_lh])

            def stageG2(ci):
                c0, n, ti = chunks[ci]
                par = (hd + ci) % 2

                def cum(e):
                    ins = None
                    for dc in range(2):
                        e.matmul(pcum[:, dc, 0:n], lhsT=lhi[0:n, dc * 128:(dc + 1) * 128], rhs=tri[0:n, 0:n], start=True, stop=False)
                        ins = e.matmul(pcum[:, dc, 0:n], lhsT=llo[0:n, dc * 128:(dc + 1) * 128], rhs=tri[0:n, 0:n], start=False, stop=True)
                    return ins
                P.op("pe", cum, [b_lh, b_const], [b3b])
                P.op("act", lambda e: e.activation(out=Eq[par][:, :, 0:n], in_=pcum[:, :, 0:n], func=AF.Exp, scale=-1.0 / 16),
                     [b3b], [bE[par]])
                P.op("act", lambda e: e.activation(out=Ek[par][:, :, 0:n], in_=pcum[:, :, 0:n], func=AF.Exp, scale=1.0 / 16),
                     [b3b], [b_Ek])

            def stageA(ci, mid=None, afterqk=None):
                c0, n, ti = chunks[ci]
                par = (hd + ci) % 2

                def mqk(e):
                    ins = None
                    for idx, Wm in ((0, Wq), (1, Wk)):
                        for dc in range(2):
                            for kc in range(8):
                                ins = e.matmul(pq[:, idx * 2 + dc, 0:n], lhsT=Wm[:, kc, dc * 128:(dc + 1) * 128],
                                               rhs=hnT[:, kc, c0:c0 + n], start=(kc == 0), stop=(kc == 7))
                    return ins
                P.op("pe", mqk, [hnTb[ti], Wb[s][0]], PB(0))
                P.op("dve", lambda e: e.scalar_tensor_tensor(out=qe[par][:, :, 0:n], in0=pq[:, 0:2, 0:n], scalar=0.0625,
                                                             in1=Eq[par][:, :, 0:n], op0=ALU.mult, op1=ALU.mult),
                     PB(0) + [bE[par]], [bQ[par]])
                P.op("dve", lambda e: e.tensor_tensor(out=ke[par][:, :, 0:n], in0=pq[:, 2:4, 0:n], in1=Ek[par][:, :, 0:n], op=ALU.mult),
                     PB(0) + [b_Ek], [bQ[par]])
                if afterqk is not None:
                    afterqk()


                def mv(e):
                    ins = None
                    for kc in range(8):
                        ins = e.matmul(pv[0:n, :], lhsT=hnT[:, kc, c0:c0 + n], rhs=Wv[:, kc, :], start=(kc == 0), stop=(kc == 7))
                    return ins
                P.op("pe", mv, [hnTb[ti], Wb[s][0]], PB(2))
                P.op("act", lambda e: e.activation(out=vb[par][0:n, :], in_=pv[0:n, :], func=AF.Copy), PB(2), [bV[par]])
                for dc in range(2):
                    P.op("act", lambda e, dc=dc: e.activation(out=kdT[par][:, dc, 0:n], in_=ke[par][:, dc, 0:n], func=AF.Copy,
                                                              scale=Eq[par][:, dc, n - 1:n]), [bQ[par], bE[par]], [bQk[par]])
                if mid is not None:
                    mid()

                def mz(e):
                    ins = None
                    for ec in range(4):
                        for kc in range(8):
                            ins = e.matmul(pz[:, ec, 0:n], lhsT=Wz[:, kc, ec * 128:(ec + 1) * 128], rhs=hnT[:, kc, c0:c0 + n],
                                           start=(kc == 0), stop=(kc == 7))
                    return ins
                P.op("pe", mz, [hnTb[ti], Wb[s][1]], PB(1))

            def stageZ(ci):
                c0, n, ti = chunks[ci]
                par = (hd + ci) % 2
                sg = sgs[par]
                P.op("act", lambda e: e.activation(out=sg[:, :, 0:n], in_=pz[:, :, 0:n], func=AF.Exp, scale=-1.0), PB(1), [bsg[par]])
                P.op("act", lambda e: e.activation(out=sg[:, :, 0:n], in_=sg[:, :, 0:n], func=AF.Ln, bias=1.0), [bsg[par]], [bsg[par]])
                P.op("act", lambda e: e.activation(out=sg[:, :, 0:n], in_=sg[:, :, 0:n], func=AF.Exp, scale=-1.0), [bsg[par]], [bsg[par]])

            def stageC(ci):
                c0, n, ti = chunks[ci]
                par = (hd + ci) % 2

                def matt(e):
                    ins = None
                    for dc in range(2):
                        ins = e.matmul(patt[0:n, 0:n], lhsT=ke[par][:, dc, 0:n], rhs=qe[par][:, dc, 0:n], start=(dc == 0), stop=(dc == 1))
                    return ins
                P.op("pe", matt, [bQ[par]], [b4a])
                P.op("dve", lambda e: e.tensor_tensor(out=attm[par][0:n, 0:n], in0=patt[0:n, 0:n], in1=tri[0:n, 0:n], op=ALU.mult),
                     [b4a, b_const], [bA[par]])

                def mkd(e):
                    ins = None
                    for dc in range(2):
                        ins = e.transpose(out=pkdT[0:n, dc * 128:(dc + 1) * 128], in_=kdT[par][:, dc, 0:n], identity=ident[:, :])
                    return ins
                P.op("pe", mkd, [bQk[par], b_const], [b4b])
                P.op("act", lambda e: e.activation(out=kd[par][0:n, :], in_=pkdT[0:n, :], func=AF.Copy), [b4b], [bK[par]])

            def stageD(ci):
                c0, n, ti = chunks[ci]
                par = (hd + ci) % 2

                def mo(e):
                    ins = None
                    for ec in range(4):
                        ins = e.matmul(po[:, ec, 0:n], lhsT=vb[par][0:n, ec * 128:(ec + 1) * 128], rhs=attm[par][0:n, 0:n],
                                       start=True, stop=(ci == 0))
                        if ci > 0:
                            for dc in range(2):
                                ins = e.matmul(po[:, ec, 0:n], lhsT=Sb[:, dc, ec * 128:(ec + 1) * 128], rhs=qe[par][:, dc, 0:n],
                                               start=False, stop=(dc == 1))
                    return ins
                P.op("pe", mo, [bV[par], bA[par], bQ[par]] + ([b_Sb] if ci > 0 else []), PB(5))
                P.op("act", lambda e: e.activation(out=o2[:, :, 0:n], in_=po[:, :, 0:n], func=AF.Square), PB(5), [b_o2])
                sg = sgs[par]
                P.op("dve", lambda e: e.tensor_tensor(out=sg[:, :, 0:n], in0=pz[:, :, 0:n], in1=sg[:, :, 0:n], op=ALU.mult),
                     PB(1) + [bsg[par]], [bsg[par]])
                for ec in range(4):
                    P.op("dve", lambda e, ec=ec: e.scalar_tensor_tensor(out=yT[par][:, ec, 0:n], in0=po[:, ec, 0:n],
                                                                        scalar=hnorm[:, hd * 4 + ec:hd * 4 + ec + 1], in1=sg[:, ec, 0:n],
                                                                        op0=ALU.mult, op1=ALU.mult),
                         PB(5) + [bsg[par], b_vec], [bYt[par]])

            def stageS(ci):
                c0, n, ti = chunks[ci]
                par = (hd + ci) % 2
                if ci < last:
                    for dc in range(2):
                        P.op("pe", lambda e, dc=dc: e.matmul(pS[dc][:, :], lhsT=kd[par][0:n, dc * 128:(dc + 1) * 128], rhs=vb[par][0:n, :],
                                                             start=True, stop=True), [bK[par], bV[par]], PB(6 + dc))
                    for dc in range(2):
                        if ci == 0:
                            P.op("dve", lambda e, dc=dc: e.tensor_copy(out=S[:, dc, :], in_=pS[dc][:, :]), PB(6 + dc), [b_S])
                        else:
                            P.op("dve", lambda e, dc=dc: e.scalar_tensor_tensor(out=S[:, dc, :], in0=S[:, dc, :],
                                                                                scalar=Eq[par][:, dc, n - 1:n], in1=pS[dc][:, :],
                                                                                op0=ALU.mult, op1=ALU.add),
                                 PB(6 + dc) + [bE[par], b_S], [b_S])

            def stageD2(ci):
                if ci < last:
                    P.op("act", lambda e: e.activation(out=Sb[:, :, :], in_=S[:, :, :], func=AF.Copy), [b_S], [b_Sb])

            def stageF(ci):
                c0, n, ti = chunks[ci]
                par = (hd + ci) % 2
                sg = sgs[par]

                def mss(e):
                    ins = None
                    for ec in range(4):
                        ins = e.matmul(pss[0:n, 0:1], lhsT=o2[:, ec, 0:n], rhs=ones[:, 0:1], start=(ec == 0), stop=(ec == 3))
                    return ins
                P.op("pe", mss, [b_o2, b_const], PB(5))
                P.op("act", lambda e: e.activation(out=rtmp[0:n, 0:1], in_=pss[0:n, 0:1], func=AF.Ln, scale=1.0 / 512, bias=epsc[0:n, 0:1]),
                     PB(5) + [b_const], [b_rs])
                P.op("act", lambda e: e.activation(out=rscol[par][0:n, 0:1], in_=rtmp[0:n, 0:1], func=AF.Exp, scale=-0.5),
                     [b_rs], [brc[par]])

            def stageE(ci):
                c0, n, ti = chunks[ci]
                par = (hd + ci) % 2
                for half in range(2):
                    pout = pS[half]

                    def mout(e, half=half, pout=pout):
                        ins = None
                        for ec in range(4):
                            ins = e.matmul(pout[0:n, :], lhsT=yT[par][:, ec, 0:n], rhs=Wo[:, ec, half * 512:(half + 1) * 512],
                                           start=(ec == 0), stop=(ec == 3))
                        return ins
                    P.op("pe", mout, [bYt[par], Wb[s][2]], PB(6 + half))
                    P.op("dve", lambda e, half=half, pout=pout: e.scalar_tensor_tensor(
                        out=h[0:n, ti, half * 512:(half + 1) * 512], in0=pout[0:n, :], scalar=rscol[par][0:n, 0:1],
                        in1=h[0:n, ti, half * 512:(half + 1) * 512], op0=ALU.mult, op1=ALU.add),
                        PB(6 + half) + [hb[ti], brc[par]], [hb[ti]])

            return dict(G=stageG, G2=stageG2, A=stageA, C=stageC, Z=stageZ, D=stageD, S=stageS, D2=stageD2, F=stageF,
                        E=stageE, ui=ui)

        heads = [head_pass(hd, ui0 + hd) for hd in range(4)]
        nch = len(chunks)
        steps = [(hd, ci) for hd in range(4) for ci in range(nch)]
        NS = len(steps)

        def call(name, k, **kw):
            hd, ci = steps[k]
            heads[hd][name](ci, **kw)

        def final_tile(i):
            P.op("act", lambda e: e.activation(out=hnb[:, :], in_=h[:, i, :], func=AF.Square, accum_out=ssb[:, i:i + 1]),
                 [hb[i]], [b_hnb, b_stat[i]])
            P.op("act", lambda e: e.activation(out=lnv[:, i:i + 1], in_=ssb[:, i:i + 1], func=AF.Ln, scale=1.0 / D,
                                               bias=epsc[:, 0:1]), [b_stat[i], b_const], [b_stat[i]])
            P.op("act", lambda e: e.activation(out=rstd[:, i:i + 1], in_=lnv[:, i:i + 1], func=AF.Exp, scale=-0.5),
                 [b_stat[i]], [b_stat[i]])

        def final_tile_b(i):
            P.op("dve", lambda e: e.scalar_tensor_tensor(out=h[:, i, :], in0=h[:, i, :], scalar=rstd[:, i:i + 1],
                                                         in1=gbc[:, :], op0=ALU.mult, op1=ALU.mult),
                 [hb[i], b_stat[i], b_gbc], [hb[i]])
            P.dma("sp", s_out2[i % 2], [(lambda q: q.dma_start(out=out[(i - 1) * 128:i * 128, :], in_=h[:, i, :]), [hb[i]], [])])

        def done_e(k):
            call("E", k)
            hd, ci = steps[k]
            if ci == nch - 1 and heads[hd]["ui"] + 2 < len(units):
                load_unit(heads[hd]["ui"] + 2)

        def maybe_final(k, part):
            hd, ci = steps[k]
            if fuse_final and hd == 3 and ci >= 1:
                (final_tile if part == 0 else final_tile_b)(ci)
            elif (not fuse_final) and li + 1 < n_layers and hd == 3 and part == 0:
                stats_sq(ci, "act" if ci % 2 == 0 else "dve")

        call("G", 0)
        call("G2", 0)
        for k in range(NS):
            def mid(k=k):
                if k + 1 < NS:
                    call("G", k + 1)
                if k >= 1:
                    call("F", k - 1)
                    call("D2", k - 1)
            def afterqk(k=k):
                call("S", k - 1)
                if k >= 2:
                    maybe_final(k - 2, 1)
            call("A", k, mid=mid, afterqk=afterqk if k >= 1 else None)
            call("C", k)
            call("Z", k)
            if k >= 1:
                done_e(k - 1)
            call("D", k)
            if k >= 1:
                maybe_final(k - 1, 0)
            if k + 1 < NS:
                call("G2", k + 1)
        call("F", NS - 1)
        done_e(NS - 1)
        maybe_final(NS - 2, 1)
        maybe_final(NS - 1, 0)
        maybe_final(NS - 1, 1)
        if (not fuse_final) and li + 1 < n_layers:
            sq_ready[0] = True

    def final():
        if n_layers % 2 == 0:
            P.wait_all("sp", hb[1:])
            return
        P.dma("sp", s_gbc, [(lambda q: q.dma_start(out=gbc[:], in_=final_norm[0:1, :].partition_broadcast(128)), [], [b_gbc])])
        junk = PS[3]
        items = []
        for i in range(1, NT):
            P.op("act", lambda e, i=i: e.activation(out=junk[:, :], in_=h[:, i, :], func=AF.Square, accum_out=ssb[:, i:i + 1]),
                 [hb[i]], PB(6, 7) + [b_stat[i]])
            P.op("act", lambda e, i=i: e.activation(out=lnv[:, i:i + 1], in_=ssb[:, i:i + 1], func=AF.Ln, scale=1.0 / D,
                                                    bias=epsc[:, 0:1]), [b_stat[i], b_const], [b_stat[i]])
            P.op("act", lambda e, i=i: e.activation(out=rstd[:, i:i + 1], in_=lnv[:, i:i + 1], func=AF.Exp, scale=-0.5),
                 [b_stat[i]], [b_stat[i]])
            P.op("dve", lambda e, i=i: e.scalar_tensor_tensor(out=h[:, i, :], in0=h[:, i, :], scalar=rstd[:, i:i + 1],
                                                              in1=gbc[:, :], op0=ALU.mult, op1=ALU.mult),
                 [hb[i], b_stat[i], b_gbc], [hb[i]])
            items.append((lambda q, i=i: q.dma_start(out=out[(i - 1) * 128:i * 128, :], in_=h[:, i, :]), [hb[i]], []))
        P.dma("sp", s_out, items)
        P.wait_all("sp", hb[1:])

    load_unit(0)
    if len(units) > 1:
        load_unit(1)
    for li in range(n_layers):
        if li % 2 == 0:
            pool_layer(li, 4 * li)
        else:
            gla_layer(li, 4 * li)
    final()
    P.emit(nc, E)
    es.close()
    return nc


_NC_CACHE = {}


def prep_inputs(x, meta, final_norm, pool_norm, pool_w_in, pool_w_grp, pool_scale, pool_w_out,
                gla_norm, gla_w_in, gla_w_gate2, gla_b_gate, gla_head_norm, gla_w_out):
    f = lambda a: np.ascontiguousarray(np.asarray(a, dtype=np.float32))
    pool_w_in, pool_w_grp, pool_w_out, gla_w_in, gla_w_out = f(pool_w_in), f(pool_w_grp), f(pool_w_out), f(gla_w_in), f(gla_w_out)
    pool_scale, gla_head_norm = f(pool_scale), f(gla_head_norm)
    t3 = lambda a, k: a.reshape(k, 128, -1).transpose(1, 0, 2).reshape(128, -1)
    wp = np.zeros((16, 128, 16384), np.float32)
    for li in range(4):
        j = li // 2
        for u in range(4):
            W = wp[4 * li + u]
            if li % 2 == 0:
                win = pool_w_in[j]
                uz = np.concatenate([win[:, u * 512:(u + 1) * 512], win[:, DI + u * 512:DI + (u + 1) * 512]], axis=1)
                W[:, 0:8192] = t3(uz, 8)
                W[:, 8192:10240] = t3(pool_w_grp[j, u], 4)
                W[:, 12288:16384] = t3(pool_w_out[j, u * 512:(u + 1) * 512], 4)
            else:
                win = gla_w_in[j]
                W[:, 0:2048] = t3(win[:, u * 256:(u + 1) * 256], 8)
                W[:, 2048:4096] = t3(win[:, 1024 + u * 256:1024 + (u + 1) * 256], 8)
                W[:, 4096:8192] = t3(win[:, 2048 + u * 512:2048 + (u + 1) * 512], 8)
                W[:, 8192:12288] = t3(win[:, 4096 + u * 512:4096 + (u + 1) * 512], 8)
                W[:, 12288:16384] = t3(gla_w_out[j, u * 512:(u + 1) * 512], 4)
    vecp = np.zeros((128, 64), np.float32)
    for j in range(2):
        vecp[:, 16 * j:16 * j + 16] = pool_scale[j].reshape(16, 128).T
        vecp[:, 32 + 16 * j:48 + 16 * j] = gla_head_norm[j].reshape(16, 128).T
    wglrp = np.stack([t3(gla_w_in[j][:, 6144:6160], 8) for j in range(2)], axis=0)
    shared = {
        "meta": f(meta), "final_norm": f(final_norm).reshape(1, D), "pool_norm": f(pool_norm), "gla_norm": f(gla_norm),
        "gla_w_gate2": f(gla_w_gate2), "gla_b_gate": f(gla_b_gate), "cst": host_consts(),
        "wpack": wp, "vecp": vecp, "wglrp": np.ascontiguousarray(wglrp),
    }
    xs = f(x)
    xr = [np.ascontiguousarray(xs[b].reshape(16, 128, D).transpose(1, 0, 2).reshape(128, 16 * D)) for b in range(xs.shape[0])]
    return shared, xr


def kernel(x, meta, final_norm, pool_norm, pool_w_in, pool_w_grp, pool_scale, pool_w_out,
           gla_norm, gla_w_in, gla_w_gate2, gla_b_gate, gla_head_norm, gla_w_out, _n_layers=4, _cores=8):
    shared, xr = prep_inputs(x, meta, final_norm, pool_norm, pool_w_in, pool_w_grp, pool_scale, pool_w_out,
                             gla_norm, gla_w_in, gla_w_gate2, gla_b_gate, gla_head_norm, gla_w_out)
    if _n_layers not in _NC_CACHE:
        _NC_CACHE[_n_layers] = build(_n_layers)
    nc = _NC_CACHE[_n_layers]
    in_maps = [dict(shared, x=xr[b]) for b in range(_cores)]
    res = run_bass_kernel_spmd(nc, in_maps, core_ids=list(range(_cores)))
    return np.stack([res.results[b]["out"] for b in range(_cores)], axis=0)
```
